# Optimizing a Trainium2 kernel written in Bass

```python
import jax, jax.numpy as jnp
from jax import lax
import numpy as np

D_MODEL = 2048
BATCH = 2
SEQ = 16384
DEPTH = 1

HEAD_DIM = 64
N_Q_HEADS = 16
N_KV_HEADS = 2
GQA_GROUP = N_Q_HEADS // N_KV_HEADS
ATTN_WIDTH = N_Q_HEADS * HEAD_DIM
KV_WIDTH = N_KV_HEADS * HEAD_DIM
WINDOW = 128
BLOCK = 128
ROPE_THETA = 10000.0
HG_HEAD_DIM = 128
HG_HEADS = 8
HG_WIDTH = HG_HEADS * HG_HEAD_DIM
CHUNK = 64
D_FF = 5632
N_BRANCH = 2
EPS = 1e-6
IN_SPLITS = (ATTN_WIDTH, KV_WIDTH, KV_WIDTH, HG_WIDTH, HG_WIDTH, HG_WIDTH, HG_WIDTH, N_BRANCH * D_MODEL)
IN_COLS = sum(IN_SPLITS)

kernel_name = 'hybrid_swa_sink_hgrn2_macaron_layer'


def rmsnorm(x, g):
    xf = x.astype(jnp.float32)
    y = xf * lax.rsqrt(jnp.mean(xf * xf, axis=-1, keepdims=True) + EPS)
    return (y * g.astype(jnp.float32)).astype(x.dtype)


def swiglu(h, w_gu, w_down):
    gate, up = jnp.split(h @ w_gu, 2, axis=-1)
    return (jax.nn.silu(gate) * up) @ w_down


def rope(t, positions):
    half = HEAD_DIM // 2
    inv_freq = ROPE_THETA ** (-jnp.arange(half, dtype=jnp.float32) * 2.0 / HEAD_DIM)
    ang = positions.astype(jnp.float32)[..., None] * inv_freq
    cos = jnp.cos(ang)[:, :, None, :]
    sin = jnp.sin(ang)[:, :, None, :]
    t1, t2 = t[..., :half], t[..., half:]
    return jnp.concatenate([t1 * cos - t2 * sin, t2 * cos + t1 * sin], axis=-1)


def sliding_window_attention(q, k, v, sinks):
    B, S = q.shape[0], q.shape[1]
    nb = S // BLOCK
    qb = q.reshape(B, nb, BLOCK, N_KV_HEADS, GQA_GROUP, HEAD_DIM)
    kb = k.reshape(B, nb, BLOCK, N_KV_HEADS, HEAD_DIM)
    vb = v.reshape(B, nb, BLOCK, N_KV_HEADS, HEAD_DIM)

    def with_prev(t):
        prev = jnp.pad(t, ((0, 0), (1, 0), (0, 0), (0, 0), (0, 0)))[:, :-1]
        return jnp.concatenate([prev, t], axis=2)

    kw, vw = with_prev(kb), with_prev(vb)
    scores = jnp.einsum('bnqkgd,bnskd->bnkgqs', qb, kw) * (HEAD_DIM ** -0.5)
    i = jnp.arange(BLOCK)[:, None]
    j = jnp.arange(2 * BLOCK)[None, :]
    band = (j <= i + BLOCK) & (j > i + BLOCK - WINDOW)
    blk = jnp.arange(nb)[:, None, None]
    valid = band[None] & (blk * BLOCK + j[None] - BLOCK >= 0)
    scores = jnp.where(valid[None, :, None, None], scores, -jnp.inf)
    sink = sinks.astype(jnp.float32).reshape(N_KV_HEADS, GQA_GROUP)[None, None, :, :, None, None]
    m = jnp.maximum(jnp.max(scores, axis=-1, keepdims=True), sink)
    p = jnp.exp(scores - m)
    denom = jnp.sum(p, axis=-1, keepdims=True) + jnp.exp(sink - m)
    out = jnp.einsum('bnkgqs,bnskd->bnqkgd', p / denom, vw)
    return out.reshape(B, S, ATTN_WIDTH)


def hgrn2(q, f_logit, inp, lb):
    B, S = q.shape[0], q.shape[1]
    nc = S // CHUNK

    def heads(t):
        return t.astype(jnp.float32).reshape(B, nc, CHUNK, HG_HEADS, HG_HEAD_DIM).transpose(0, 3, 1, 2, 4)

    f = lb + (1.0 - lb) * jax.nn.sigmoid(f_logit.astype(jnp.float32))
    qh = heads(jax.nn.silu(q.astype(jnp.float32)))
    kh = heads(1.0 - f)
    vh = heads(inp)
    b = jnp.cumsum(heads(jnp.log(f)), axis=3)
    b_ref = b[:, :, :, CHUNK // 2:CHUNK // 2 + 1]
    attn = jnp.einsum('bhncd,bhnsd->bhncs', qh * jnp.exp(b - b_ref), kh * jnp.exp(b_ref - b))
    causal = jnp.tril(jnp.ones((CHUNK, CHUNK), dtype=bool))
    o_intra = jnp.einsum('bhncs,bhnse->bhnce', jnp.where(causal, attn, 0.0), vh)
    b_last = b[:, :, :, -1:]
    upd = jnp.einsum('bhnsd,bhnse->bhnde', kh * jnp.exp(b_last - b), vh)
    decay = jnp.exp(b_last[:, :, :, 0])

    def step(state, xs):
        u_c, d_c = xs
        return d_c[..., None] * state + u_c, state

    s0 = jnp.zeros((B, HG_HEADS, HG_HEAD_DIM, HG_HEAD_DIM), jnp.float32)
    _, s_prev = lax.scan(step, s0, (jnp.moveaxis(upd, 2, 0), jnp.moveaxis(decay, 2, 0)))
    s_prev = jnp.moveaxis(s_prev, 0, 2)
    o_inter = jnp.einsum('bhncd,bhnde->bhnce', qh * jnp.exp(b), s_prev)
    o = o_intra + o_inter
    return o.transpose(0, 2, 3, 1, 4).reshape(B, S, HG_HEADS, HG_HEAD_DIM)


def setup_inputs(seed: int = 0) -> dict:
    key = jax.random.key(seed)
    ks = jax.random.split(key, 20)

    def w(k, shape, fan_in):
        return jax.random.normal(k, shape, jnp.float32) * (fan_in ** -0.5)

    def gain(k, shape):
        return 1.0 + 0.05 * jax.random.normal(k, shape, jnp.float32)

    return {
        'x': jax.random.normal(ks[0], (BATCH, SEQ, D_MODEL), jnp.float32),
        'positions': jnp.tile(jnp.arange(SEQ, dtype=jnp.int32)[None, :], (BATCH, 1)),
        'lb_table': 0.1 * jax.random.normal(ks[1], (DEPTH + 1, HG_WIDTH), jnp.float32),
        'ffn1_norm': gain(ks[2], (DEPTH, D_MODEL)),
        'ffn1_w_gu': w(ks[3], (DEPTH, D_MODEL, 2 * D_FF), D_MODEL),
        'ffn1_w_down': w(ks[4], (DEPTH, D_FF, D_MODEL), D_FF),
        'mix_norm': gain(ks[5], (DEPTH, D_MODEL)),
        'w_in': w(ks[6], (DEPTH, D_MODEL, IN_COLS), D_MODEL),
        'q_norm': gain(ks[7], (DEPTH, HEAD_DIM)),
        'k_norm': gain(ks[8], (DEPTH, HEAD_DIM)),
        'sinks': 0.5 * jax.random.normal(ks[9], (DEPTH, N_Q_HEADS), jnp.float32),
        'hg_out_norm': gain(ks[10], (DEPTH, HG_HEAD_DIM)),
        'w_attn_branch': w(ks[11], (DEPTH, ATTN_WIDTH, D_MODEL), ATTN_WIDTH),
        'w_hg_branch': w(ks[12], (DEPTH, HG_WIDTH, D_MODEL), HG_WIDTH),
        'w_out': w(ks[13], (DEPTH, D_MODEL, D_MODEL), D_MODEL),
        'ffn2_norm': gain(ks[14], (DEPTH, D_MODEL)),
        'ffn2_w_gu': w(ks[15], (DEPTH, D_MODEL, 2 * D_FF), D_MODEL),
        'ffn2_w_down': w(ks[16], (DEPTH, D_FF, D_MODEL), D_FF),
    }


def reference(x, positions, lb_table, ffn1_norm, ffn1_w_gu, ffn1_w_down, mix_norm, w_in, q_norm, k_norm,
              sinks, hg_out_norm, w_attn_branch, w_hg_branch, w_out, ffn2_norm, ffn2_w_gu, ffn2_w_down):
    B, S = x.shape[0], x.shape[1]
    lbs = jnp.cumsum(jax.nn.softmax(lb_table.astype(jnp.float32), axis=0), axis=0)
    split_idx = np.cumsum(IN_SPLITS)[:-1].tolist()
    for l in range(DEPTH):
        x = x + 0.5 * swiglu(rmsnorm(x, ffn1_norm[l]), ffn1_w_gu[l], ffn1_w_down[l])
        h = rmsnorm(x, mix_norm[l])
        z = h @ w_in[l]
        a_q, a_k, a_v, g_q, g_f, g_i, g_o, br = jnp.split(z, split_idx, axis=-1)
        qa = rmsnorm(a_q.astype(jnp.float32).reshape(B, S, N_Q_HEADS, HEAD_DIM), q_norm[l])
        ka = rmsnorm(a_k.astype(jnp.float32).reshape(B, S, N_KV_HEADS, HEAD_DIM), k_norm[l])
        va = a_v.astype(jnp.float32).reshape(B, S, N_KV_HEADS, HEAD_DIM)
        y_attn = sliding_window_attention(rope(qa, positions), rope(ka, positions), va, sinks[l]).astype(x.dtype)
        o_hg = hgrn2(g_q, g_f, g_i, lbs[l])
        gate_hg = jax.nn.silu(g_o.astype(jnp.float32)).reshape(B, S, HG_HEADS, HG_HEAD_DIM)
        y_hg = (rmsnorm(o_hg, hg_out_norm[l]) * gate_hg).reshape(B, S, HG_WIDTH).astype(x.dtype)
        gates = jax.nn.sigmoid(br).reshape(B, S, N_BRANCH, D_MODEL)
        merged = gates[:, :, 0] * (y_attn @ w_attn_branch[l]) + gates[:, :, 1] * (y_hg @ w_hg_branch[l])
        x = x + merged @ w_out[l]
        x = x + 0.5 * swiglu(rmsnorm(x, ffn2_norm[l]), ffn2_w_gu[l], ffn2_w_down[l])
    return x
```

```python
import numpy as np
import concourse.bass as bass
import concourse.mybir as mybir
from concourse.bass_utils import run_bass_kernel_spmd

F32 = mybir.dt.float32
BF16 = mybir.dt.bfloat16
I32 = mybir.dt.int32
AF = mybir.ActivationFunctionType
ALU = mybir.AluOpType

D = 2048
DFF = 5632
KC = D // 128
FC = DFF // 128
SEQ = 16384
SEG = 4096
HALO = 128
NT = SEG + HALO
EPS = 1e-6
TMAX = 512
NSLOT = 4
SLOT_ELEMS = 8192
OQ, OK_, OV, OGQ, OGF, OGI, OGO, OBR = 0, 1024, 1152, 1280, 2304, 3328, 4352, 5376
TWO_PI_HI = 6.28125
TWO_PI_LO = 2.0 * np.pi - 6.28125
PI_CLAMP = 3.1415925


def pipeline(gens, depth=2, offset=0):
    it = iter(gens)
    active = []
    while True:
        while len(active) < depth:
            g = next(it, None)
            if g is None:
                break
            if active:
                for _ in range(offset):
                    for a in list(active):
                        try:
                            next(a)
                        except StopIteration:
                            active.remove(a)
            active.append(g)
        if not active:
            break
        for a in list(active):
            try:
                next(a)
            except StopIteration:
                active.remove(a)


class Buf:
    __slots__ = ("ap", "keys")

    def __init__(self, ap, keys):
        self.ap = ap
        self.keys = tuple(keys)


def _keys(items):
    out = []
    for it in items:
        if isinstance(it, Buf):
            out.extend(it.keys)
        else:
            out.append(it)
    return out


class Sched:
    ENGS = ("pe", "act", "dve", "pool", "sp")

    def __init__(self, nc):
        self.nc = nc
        self.ops = {e: [] for e in self.ENGS}
        self.lastw = {}
        self.readers = {}
        self.dma_cnt = {}

    def add(self, eng, fn, reads=(), writes=(), dma=None):
        rk = _keys(reads)
        wk = _keys(writes)
        deps = {}

        def need(ev):
            k = (ev[0], ev[1])
            if deps.get(k, -1) < ev[2]:
                deps[k] = ev[2]

        for k in rk:
            ev = self.lastw.get(k)
            if ev is not None:
                need(ev)
        for k in wk:
            ev = self.lastw.get(k)
            if ev is not None:
                need(ev)
            rd = self.readers.get(k)
            if rd:
                for kk, vv in rd.items():
                    need((kk[0], kk[1], vv))
        if eng == "pe":
            deps.pop(("c", "pe"), None)
        idx = len(self.ops[eng])
        if dma is not None:
            c = self.dma_cnt.get(dma, 0) + 1
            self.dma_cnt[dma] = c
            ev = ("d", dma, 16 * c)
        else:
            ev = ("c", eng, idx)
        for (ty, who), v in deps.items():
            if ty == "c":
                self.ops[who][v][3] = True
        self.ops[eng].append([fn, deps, dma, False])
        for k in wk:
            self.lastw[k] = ev
            self.readers[k] = {}
        wset = set(wk)
        for k in rk:
            if k in wset:
                continue
            rd = self.readers.setdefault(k, {})
            kk = (ev[0], ev[1])
            if rd.get(kk, -1) < ev[2]:
                rd[kk] = ev[2]
        return ev

    def emit(self):
        nc = self.nc
        sems = {e: nc.alloc_semaphore("s_" + e) for e in ("pe", "act", "dve", "pool")}
        dsem = {k: nc.alloc_semaphore("d_%d" % i) for i, k in enumerate(self.dma_cnt)}
        cum = {}
        for e, ops in self.ops.items():
            c = 0
            arr = []
            for o in ops:
                if o[3] and o[2] is None:
                    c += 1
                arr.append(c)
            cum[e] = arr
        ops_all = self.ops

        def run(e, engine):
            seen = {}
            for fn, deps, dma, sig in ops_all[e]:
                for (ty, who), v in deps.items():
                    if ty == "c":
                        s = sems[who]
                        val = cum[who][v]
                    else:
                        s = dsem[who]
                        val = v
                    if seen.get((ty, who), 0) >= val:
                        continue
                    seen[(ty, who)] = val
                    engine.wait_ge(s, val)
                ins = fn(engine)
                if dma is not None:
                    ins.then_inc(dsem[dma], 16)
                elif sig:
                    ins.then_inc(sems[e], 1)

        with nc.Block() as block:
            @block.tensor
            def _(eng):
                run("pe", eng)

            @block.scalar
            def _(eng):
                run("act", eng)

            @block.vector
            def _(eng):
                run("dve", eng)

            @block.gpsimd
            def _(eng):
                run("pool", eng)

            @block.sync
            def _(eng):
                run("sp", eng)


def build(n_tiles=9):
    nc = bass.Bass("TRN2", target_bir_lowering=False)
    S = Sched(nc)

    def din(name, shape, dt=F32):
        return nc.dram_tensor(name, list(shape), dt, kind="ExternalInput").ap()

    xT = din("xT", [D, NT])
    pos_d = din("pos", [1, NT], I32)
    wgu = [din("wgu1", [D, 2 * DFF]), din("wgu2", [D, 2 * DFF])]
    wdn = [din("wd1", [DFF, D]), din("wd2", [DFF, D])]
    win = din("win", [D, 9472])
    wA = din("wA", [1024, D])
    wR = din("wR", [1024, D])
    wO = din("wO", [D, D])
    gains_d = din("gains", [128, 3, KC])
    qkg_d = din("qkg", [128, 2])
    hgg_d = din("hgg", [128, 1])
    lbt_d = din("lbt", [128, 2, 8])
    sinks_d = din("sinks", [1, 16])
    cmat_d = din("cmat", [128, 5, 128])
    amask_d = din("amask", [128, 2, 1024])
    rmask_d = din("rmask", [128, 512])
    ropec_d = din("ropec", [128, 1])
    outT = nc.dram_tensor("outT", [D, SEG], F32, kind="ExternalOutput").ap()

    def sb(name, shape, dt):
        return nc.alloc_sbuf_tensor("sb_" + name, shape, dt)
    x32 = sb("x32", [128, KC, TMAX], F32)
    hbf = sb("hbf", [128, KC, TMAX], BF16)
    act = sb("act", [128, FC, TMAX], BF16)
    wsl = [sb("wsl%d" % i, [128, SLOT_ELEMS], BF16) for i in range(NSLOT)]
    Ft = [sb("ftmp%d" % i, [128, TMAX], F32) for i in range(5)]
    sqr = sb("sqr", [128, 2, TMAX], BF16)
    cm32 = sb("cm32", [128, 5, 128], F32)
    cmbf = sb("cmbf", [128, 5, 128], BF16)
    amask = sb("amask", [128, 2, 1024], BF16)
    rmask = sb("rmask", [128, 512], F32)
    cosT = sb("cosT", [128, TMAX], F32)
    sinT = sb("sinT", [128, TMAX], F32)
    posi = sb("posi", [128, TMAX], I32)
    gains = sb("gains", [128, 3, KC], F32)
    qkg = sb("qkg", [128, 2], F32)
    hgg = sb("hgg", [128, 1], F32)
    lbt = sb("lbt", [128, 2, 8], F32)
    lbv = sb("lbv", [128, 2, 8], F32)
    esink = sb("esink", [128, 8], F32)
    ropec = sb("ropec", [128, 1], F32)
    cvec = sb("cvec", [128, 4], F32)
    kT = sb("kT", [128, 5 * 128], BF16)
    Vpad = sb("Vpad", [128, 5, 2, 128], BF16)
    onespad = sb("onespad", [128, 2, 128], BF16)
    S32 = sb("S32", [128, 8, 128], F32)
    Sbf = sb("Sbf", [128, 8, 128], BF16)
    KtT = sb("KtT", [128, 4, 128], BF16)
    Qh = sb("Qh", [128, TMAX], BF16)
    Kt = sb("Kt", [128, TMAX], BF16)
    ATm = sb("ATm", [128, 4, 128], BF16)
    Sring = sb("Sring", [128, 2, 2, 128], BF16)
    KtT2 = sb("KtT2", [128, 4, 128], BF16)
    Qh2 = sb("Qh2", [128, TMAX], BF16)
    Kt2 = sb("Kt2", [128, TMAX], BF16)
    ATm2 = sb("ATm2", [128, 4, 128], BF16)
    Kh = [sb("Khat0", [128, TMAX], BF16), sb("Khat1", [128, TMAX], BF16)]

    PS = [nc.alloc_psum_tensor("psd%d" % i, [128, 1024], F32) for i in range(4)]

    def bank(b):
        return PS[b // 2][:, (b % 2) * 512:(b % 2) * 512 + 512]

    BK = [Buf(bank(b), [("ps", b)]) for b in range(8)]

    X = [Buf(x32[:, c, :], [("x", c)]) for c in range(KC)]
    H = [Buf(hbf[:, c, :], [("h", c)]) for c in range(KC)]
    Hall = Buf(None, [("h", c) for c in range(KC)])
    A = [Buf(act[:, j, :], [("a", j)]) for j in range(FC)]
    WS = [Buf(wsl[i][:], [("w", i)]) for i in range(NSLOT)]
    FT = [Buf(Ft[i][:], [("f", i)]) for i in range(5)]
    SQ = [Buf(sqr[:, i, :], [("sq", i)]) for i in range(2)]
    CONST = Buf(None, ["const"])
    COS = Buf(cosT[:], ["cos"])
    SIN = Buf(sinT[:], ["sin"])
    POSI = Buf(posi[:], ["posi"])
    KTB = [Buf(kT[:, b * 128:(b + 1) * 128], [("kT", b)]) for b in range(5)]
    VPB = [Buf(Vpad[:, b, :, :], [("vp", b)]) for b in range(5)]
    SB32 = [Buf(S32[:, h, :], [("s32", h)]) for h in range(8)]
    SBF = [Buf(Sbf[:, h, :], [("sbf", h)]) for h in range(8)]
    KTT = [Buf(KtT[:, b, :], [("ktt", b)]) for b in range(4)]
    QH = Buf(Qh[:], ["qh"])
    KT_ = Buf(Kt[:], ["kt"])
    ATB = [Buf(ATm[:, b, :], [("atm", b)]) for b in range(4)]
    SRB = [[Buf(Sring[:, p_, r_, :], [("sring", p_, r_)]) for r_ in range(2)] for p_ in range(2)]
    HGSET = [
        dict(KtT=KtT, KTT=KTT, Qh=Qh, QH=QH, Kt=Kt, KT=KT_, ATm=ATm, ATB=ATB),
        dict(KtT=KtT2, KTT=[Buf(KtT2[:, b, :], [("ktt2", b)]) for b in range(4)], Qh=Qh2, QH=Buf(Qh2[:], ["qh2"]),
             Kt=Kt2, KT=Buf(Kt2[:], ["kt2"]), ATm=ATm2, ATB=[Buf(ATm2[:, b, :], [("atm2", b)]) for b in range(4)]),
    ]

    def f32view(j):
        v = act[:, j:j + 2, :].bitcast(F32).rearrange("p a t -> p (a t)")
        return Buf(v, [("a", j), ("a", j + 1)])

    HT = [f32view(32 + 2 * i) for i in range(6)]
    HTSET = [HT[0:5], [HT[5]] + [f32view(2 * i) for i in range(4)]]
    Eviews = [act[:, 40:44, :].rearrange("p (g k) t -> p g (k t)", g=2),
              act[:, 36:40, :].rearrange("p (g k) t -> p g (k t)", g=2)]
    EBS = [[Buf(Eviews[0][:, g, :], [("a", 40 + 2 * g), ("a", 41 + 2 * g)]) for g in range(2)],
           [Buf(Eviews[1][:, g, :], [("a", 36 + 2 * g), ("a", 37 + 2 * g)]) for g in range(2)]]
    pstrs = [PS[2][:, 0:512].bitcast(BF16), PS[2][:, 512:1024].bitcast(BF16)]
    rot_state = {"sc": 0, "pj": 0}
    vhv = act[:, 16:24, :].rearrange("p (b a) t -> p b (a t)", b=4)
    VH = [Buf(vhv[:, b, :], [("a", 16 + 2 * b), ("a", 17 + 2 * b)]) for b in range(4)]
    pstr = PS[3][:, 512:1024].bitcast(BF16)

    ident_bf = cmbf[:, 0, :]
    ones_bf = cmbf[:, 1, :]
    hgmask_bf = cmbf[:, 3, :]
    rot32 = cm32[:, 2, :]
    hblk32 = cm32[:, 4, :]
    ones32 = cm32[:, 1, :]

    def ld(dst, src, eng="sp", sem="c0"):
        S.add(eng, lambda e: e.dma_start(out=dst, in_=src), writes=[CONST], dma=sem)

    ld(cm32[:], cmat_d)
    ld(rmask[:], rmask_d)
    ld(gains[:], gains_d)
    ld(qkg[:], qkg_d)
    ld(hgg[:], hgg_d)
    ld(lbt[:], lbt_d)
    ld(ropec[:], ropec_d)
    ld(esink[0:64, :], sinks_d[:, 0:8].partition_broadcast(64))
    ld(esink[64:128, :], sinks_d[:, 8:16].partition_broadcast(64))
    ld(cmbf[:], cmat_d, eng="pool", sem="c1")
    ld(amask[:], amask_d, eng="pool", sem="c1")
    C0 = Buf(None, ["const"])
    S.add("dve", lambda e: e.memset(cvec[:, 0:1], EPS), writes=["cv0"])
    S.add("dve", lambda e: e.memset(cvec[:, 1:2], float(np.pi / 2)), writes=["cv1"])
    S.add("dve", lambda e: e.memset(S32[:], 0.0), writes=SB32)
    S.add("dve", lambda e: e.memset(Sbf[:], 0.0), writes=SBF)
    S.add("dve", lambda e: e.memset(Vpad[:], 0.0), writes=VPB)
    S.add("dve", lambda e: e.memset(kT[:], 0.0), writes=KTB)
    S.add("dve", lambda e: e.memset(onespad[:], 0.0), writes=["onespad"])
    S.add("dve", lambda e: e.memset(onespad[:, 0, 0:64], 1.0), reads=[], writes=["onespad"])
    S.add("dve", lambda e: e.memset(onespad[:, 1, 64:128], 1.0), reads=[], writes=["onespad"])
    S.add("dve", lambda e: e.tensor_tensor(out=lbv[:, 1, :], in0=lbt[:, 0, :], in1=lbt[:, 1, :], op=ALU.subtract),
          reads=[C0], writes=["lbv1"])
    S.add("act", lambda e: e.activation(out=lbv[:, 0, :], in_=lbv[:, 1, :], func=AF.Sigmoid),
          reads=["lbv1"], writes=["lbv0"])
    S.add("dve", lambda e: e.tensor_scalar(out=lbv[:, 1, :], in0=lbv[:, 0, :], scalar1=-1.0, scalar2=1.0,
                                           op0=ALU.mult, op1=ALU.add), reads=["lbv0"], writes=["lbv1"])
    S.add("act", lambda e: e.activation(out=esink[:], in_=esink[:], func=AF.Exp), reads=[C0], writes=["esink"])
    LBV = Buf(None, ["lbv0", "lbv1"])
    CV = Buf(None, ["cv0", "cv1"])

    stages = []

    def view3(base, a, b):
        return base[:, 0:a * b].rearrange("p (a b) -> p a b", a=a)

    def wview3(slot, a, b):
        return view3(wsl[slot], a, b)

    def tile_prog(ti, t0, T, halo):
        nblk = T // 128
        o0 = t0 - HALO

        has_next = (ti + 1 < n_tiles)
        tn0 = t0 + T

        def load_x_chunk(c, c0, Tn, q="sp"):
            S.add(q, lambda e: e.dma_start(out=x32[:, c, 0:Tn], in_=xT[c * 128:(c + 1) * 128, c0:c0 + Tn]),
                  writes=[X[c]], dma=("xld", c))

        def load_pos(c0, Tn, q="sp"):
            S.add(q, lambda e: e.dma_start(out=posi[:, 0:Tn], in_=pos_d[:, c0:c0 + Tn].partition_broadcast(128)),
                  writes=[POSI], dma="pld")

        if ti == 0:
            def st_load(_):
                for c in range(KC):
                    load_x_chunk(c, t0, T)
                load_pos(t0, T)
            stages.append((None, st_load))

        def norm(gi):
            def fn(_):
                for c in range(KC):
                    r = c % 2
                    S.add("act", lambda e, c=c, r=r: e.activation(out=sqr[:, r, 0:T], in_=x32[:, c, 0:T], func=AF.Square),
                          reads=[X[c]], writes=[SQ[r]])
                    S.add("pe", lambda e, c=c, r=r: e.matmul(bank(0)[:, 0:T], ones_bf, sqr[:, r, 0:T],
                                                             start=(c == 0), stop=(c == KC - 1)),
                          reads=[SQ[r], CONST], writes=[BK[0]])
                S.add("act", lambda e: e.activation(out=Ft[2][:, 0:T], in_=bank(0)[:, 0:T], func=AF.Ln,
                                                    scale=1.0 / D, bias=cvec[:, 0:1]),
                      reads=[BK[0], CV], writes=[FT[2]])
                S.add("act", lambda e: e.activation(out=Ft[2][:, 0:T], in_=Ft[2][:, 0:T], func=AF.Exp, scale=-0.5),
                      reads=[FT[2]], writes=[FT[2]])
                for c in range(KC):
                    q = "dve"
                    S.add(q, lambda e, c=c: e.scalar_tensor_tensor(out=hbf[:, c, 0:T], in0=x32[:, c, 0:T],
                                                                   scalar=gains[:, gi, c:c + 1], in1=Ft[2][:, 0:T],
                                                                   op0=ALU.mult, op1=ALU.mult),
                          reads=[X[c], FT[2], CONST], writes=[H[c]])
            return fn

        def ffn(which):
            stages.append((None, norm(0 if which == 0 else 2)))
            wg = wgu[which]
            wd = wdn[which]
            wg3 = wg.rearrange("(c p) n -> p c n", p=128)
            for jp in range(FC // 2):
                pieces = [((KC, 512, 0, 256), wg3[:, :, jp * 256:jp * 256 + 256]),
                          ((KC, 512, 256, 512), wg3[:, :, DFF + jp * 256:DFF + jp * 256 + 256])]

                def fn(slot, jp=jp):
                    w3 = wview3(slot, KC, 512)
                    if jp == 0 and T == 512:
                        for kc in range(KC):
                            def mmk(e, kc=kc):
                                ins = None
                                for i in range(2):
                                    for (bb, off) in ((2 * i, i * 128), (2 * i + 1, 256 + i * 128)):
                                        ins = e.matmul(bank(bb)[:, 0:T], w3[:, kc, off:off + 128], hbf[:, kc, 0:T],
                                                       start=(kc == 0), stop=(kc == KC - 1))
                                return ins
                            S.add("pe", mmk, reads=[WS[slot], H[kc]], writes=[BK[0], BK[1], BK[2], BK[3]])
                    for i in range(2):
                        j = 2 * jp + i
                        bg = (jp % 2) * 4 + 2 * i
                        bu = bg + 1
                        for (bb, off) in ((bg, i * 128), (bu, 256 + i * 128)):
                            if jp == 0 and T == 512:
                                continue
                            def mm(e, bb=bb, off=off):
                                ins = None
                                for kc in range(KC):
                                    ins = e.matmul(bank(bb)[:, 0:T], w3[:, kc, off:off + 128], hbf[:, kc, 0:T],
                                                   start=(kc == 0), stop=(kc == KC - 1))
                                return ins
                            S.add("pe", mm, reads=[WS[slot], Hall], writes=[BK[bb]])
                        fi = i
                        S.add("act", lambda e, bg=bg, fi=fi: e.activation(out=Ft[fi][:, 0:T], in_=bank(bg)[:, 0:T], func=AF.Silu),
                              reads=[BK[bg]], writes=[FT[fi]])
                        S.add("dve", lambda e, bu=bu, fi=fi, j=j: e.tensor_tensor(out=act[:, j, 0:T], in0=Ft[fi][:, 0:T],
                                                                                in1=bank(bu)[:, 0:T], op=ALU.mult),
                              reads=[FT[fi], BK[bu]], writes=[A[j]])
                stages.append((pieces, fn, ("gu", which, jp)))
            wd3 = wd.rearrange("(k p) n -> p k n", p=128)
            for mg in range(8):
                for kh in range(2):
                    pieces = [((22, 256, 0, 256), wd3[:, kh * 22:(kh + 1) * 22, mg * 256:(mg + 1) * 256])]

                    def fn(slot, mg=mg, kh=kh):
                        w3 = wview3(slot, 22, 256)
                        for mi in range(2):
                            bb = (mg % 2) * 2 + mi
                            m = mg * 2 + mi

                            def mm(e, bb=bb, mi=mi):
                                ins = None
                                for k in range(22):
                                    ins = e.matmul(bank(bb)[:, 0:T], w3[:, k, mi * 128:(mi + 1) * 128], act[:, kh * 22 + k, 0:T],
                                                   start=(kh == 0 and k == 0), stop=(kh == 1 and k == 21))
                                return ins
                            S.add("pe", mm, reads=[WS[slot]] + A[kh * 22:(kh + 1) * 22], writes=[BK[bb]])
                            if kh == 1:
                                S.add("dve", lambda e, bb=bb, m=m: e.scalar_tensor_tensor(
                                    out=x32[:, m, 0:T], in0=bank(bb)[:, 0:T], scalar=0.5, in1=x32[:, m, 0:T],
                                    op0=ALU.mult, op1=ALU.add), reads=[BK[bb], X[m]], writes=[X[m]])
                                if which == 1:
                                    S.add("pool", lambda e, m=m: e.dma_start(out=outT[m * 128:(m + 1) * 128, o0:o0 + T], in_=x32[:, m, 0:T]),
                                          reads=[X[m]], writes=[("outdram", m)], dma=("ost", m))
                                    if has_next:
                                        load_x_chunk(m, tn0, 512, "pool")
                                        if m == KC - 1:
                                            load_pos(tn0, 512, "pool")
                    stages.append((pieces, fn, ("dn", which, mg, kh)))

        ffn(0)
        stages.append((None, norm(1)))
        if halo and has_next:
            def st_load_next(_):
                for c in range(KC):
                    load_x_chunk(c, tn0, 512)
            stages.append((None, st_load_next))

        def st_rope(_):
            a = Ft[3]
            k_ = Ft[4]
            AB = FT[3]
            KB = FT[4]
            S.add("dve", lambda e: e.tensor_copy(out=a[:, 0:T], in_=posi[:, 0:T]), reads=[POSI], writes=[AB])
            S.add("dve", lambda e: e.tensor_scalar(out=a[:, 0:T], in0=a[:, 0:T], scalar1=ropec[:, 0:1], scalar2=None,
                                                   op0=ALU.mult), reads=[AB, CONST], writes=[AB])
            S.add("dve", lambda e: e.tensor_scalar(out=k_[:, 0:T], in0=a[:, 0:T], scalar1=float(1.0 / (2 * np.pi)),
                                                   scalar2=12582912.0, op0=ALU.mult, op1=ALU.add), reads=[AB], writes=[KB])
            S.add("dve", lambda e: e.tensor_scalar(out=k_[:, 0:T], in0=k_[:, 0:T], scalar1=12582912.0, scalar2=None,
                                                   op0=ALU.subtract), reads=[KB], writes=[KB])
            S.add("dve", lambda e: e.scalar_tensor_tensor(out=a[:, 0:T], in0=k_[:, 0:T], scalar=-TWO_PI_HI, in1=a[:, 0:T],
                                                          op0=ALU.mult, op1=ALU.add), reads=[KB, AB], writes=[AB])
            S.add("dve", lambda e: e.scalar_tensor_tensor(out=a[:, 0:T], in0=k_[:, 0:T], scalar=-float(TWO_PI_LO), in1=a[:, 0:T],
                                                          op0=ALU.mult, op1=ALU.add), reads=[KB, AB], writes=[AB])
            S.add("dve", lambda e: e.tensor_scalar(out=a[:, 0:T], in0=a[:, 0:T], scalar1=-PI_CLAMP, scalar2=PI_CLAMP,
                                                   op0=ALU.max, op1=ALU.min), reads=[AB], writes=[AB])
            S.add("act", lambda e: e.activation(out=sinT[:, 0:T], in_=a[:, 0:T], func=AF.Sin), reads=[AB], writes=[SIN])
            S.add("dve", lambda e: e.scalar_tensor_tensor(out=k_[:, 0:T], in0=a[:, 0:T], scalar=-1.0, in1=a[:, 0:T], op0=ALU.mult, op1=ALU.max),
                  reads=[AB], writes=[KB])
            S.add("act", lambda e: e.activation(out=cosT[:, 0:T], in_=k_[:, 0:T], func=AF.Sin, scale=-1.0, bias=cvec[:, 1:2]),
                  reads=[KB, CV], writes=[COS])
        stages.append((None, st_rope))
        if halo and has_next:
            stages.append((None, lambda _: load_pos(tn0, 512)))

        win3 = win.rearrange("(c p) n -> p c n", p=128)

        def proj(slot, w3, off, bb):
            def mm(e):
                ins = None
                for kc in range(KC):
                    ins = e.matmul(bank(bb)[:, 0:T], w3[:, kc, off:off + 128], hbf[:, kc, 0:T],
                                   start=(kc == 0), stop=(kc == KC - 1))
                return ins
            S.add("pe", mm, reads=[WS[slot], Hall], writes=[BK[bb]])

        def qk_chain(slot, w3, off, bb, gcol, dst_ap, dst_buf, par, preproj=False):
            if par == 0:
                zb, q2b = FT[0], FT[1]
                z, q2 = Ft[0], Ft[1]
            else:
                zb, q2b = HT[0], HT[1]
                z, q2 = HT[0].ap, HT[1].ap
            sb2 = 4 + (bb % 2)
            if not preproj:
                proj(slot, w3, off, bb)
                yield
            S.add("act", lambda e: e.activation(out=z[:, 0:T], in_=bank(bb)[:, 0:T], func=AF.Copy), reads=[BK[bb]], writes=[zb])
            yield
            S.add("act", lambda e: e.activation(out=q2[:, 0:T], in_=bank(bb)[:, 0:T], func=AF.Square), reads=[BK[bb]], writes=[q2b])
            yield
            S.add("pe", lambda e: e.matmul(bank(sb2)[:, 0:T], hblk32, q2[:, 0:T], start=True, stop=True),
                  reads=[q2b, CONST], writes=[BK[sb2]])
            yield
            S.add("act", lambda e: e.activation(out=q2[:, 0:T], in_=bank(sb2)[:, 0:T], func=AF.Ln, scale=1.0 / 64, bias=cvec[:, 0:1]),
                  reads=[BK[sb2], CV], writes=[q2b])
            yield
            S.add("act", lambda e: e.activation(out=q2[:, 0:T], in_=q2[:, 0:T], func=AF.Exp, scale=-0.5), reads=[q2b], writes=[q2b])
            yield
            S.add("dve", lambda e: e.scalar_tensor_tensor(out=z[:, 0:T], in0=z[:, 0:T], scalar=qkg[:, gcol:gcol + 1], in1=q2[:, 0:T],
                                                          op0=ALU.mult, op1=ALU.mult), reads=[zb, q2b, CONST], writes=[zb])
            yield
            S.add("pe", lambda e: e.matmul(bank(sb2 + 2)[:, 0:T], rot32, z[:, 0:T], start=True, stop=True),
                  reads=[zb, CONST], writes=[BK[sb2 + 2]])
            yield
            S.add("dve", lambda e: e.tensor_tensor(out=q2[:, 0:T], in0=bank(sb2 + 2)[:, 0:T], in1=sinT[:, 0:T], op=ALU.mult),
                  reads=[BK[sb2 + 2], SIN], writes=[q2b])
            yield
            S.add("dve", lambda e: e.tensor_tensor(out=z[:, 0:T], in0=z[:, 0:T], in1=cosT[:, 0:T], op=ALU.mult),
                  reads=[zb, COS], writes=[zb])
            yield
            S.add("dve", lambda e: e.tensor_tensor(out=dst_ap, in0=z[:, 0:T], in1=q2[:, 0:T], op=ALU.add),
                  reads=[zb, q2b], writes=[dst_buf])
            yield

        if not halo:
            wq5 = win[:, OQ:OQ + 1024].rearrange("(kc p) (g c d) -> p kc c g d", p=128, g=2, c=8)
            for qg in range(2):
                pieces = []
                for ci in range(4):
                    for g in range(2):
                        pieces.append(((KC, 512, ci * 128 + g * 64, ci * 128 + g * 64 + 64), wq5[:, :, qg * 4 + ci, g, :]))

                def fn(slot, qg=qg):
                    w3 = wview3(slot, KC, 512)
                    pre = (qg == 0)
                    if pre:
                        for kc in range(KC):
                            def mmk(e, kc=kc):
                                ins = None
                                for ci in range(2):
                                    ins = e.matmul(bank(ci)[:, 0:T], w3[:, kc, ci * 128:(ci + 1) * 128], hbf[:, kc, 0:T],
                                                   start=(kc == 0), stop=(kc == KC - 1))
                                return ins
                            S.add("pe", mmk, reads=[WS[slot], H[kc]], writes=[BK[0], BK[1]])
                    pipeline([qk_chain(slot, w3, ci * 128, ci % 2, 0, act[:, qg * 4 + ci, 0:T], A[qg * 4 + ci], ci % 2,
                                       preproj=(pre and ci < 2))
                              for ci in range(4)], depth=2, offset=3)
                stages.append((pieces, fn, ("q", qg)))

        pieces = [((KC, 256, 0, 256), win3[:, :, OK_:OK_ + 256])]

        def fn_kv(slot):
            w3 = wview3(slot, KC, 256)
            kdst = kT[:, 128:128 + T]
            for _ in qk_chain(slot, w3, 0, 0, 1, kdst, Buf(None, [("kT", 1 + b) for b in range(nblk)]), 0):
                pass
            for b in range(nblk):
                bb = 2 + (b % 2)

                def mm(e, b=b, bb=bb):
                    ins = None
                    for kc in range(KC):
                        ins = e.matmul(bank(bb)[:, 0:128], hbf[:, kc, b * 128:(b + 1) * 128], w3[:, kc, 128:256],
                                       start=(kc == 0), stop=(kc == KC - 1))
                    return ins
                S.add("pe", mm, reads=[WS[slot], Hall], writes=[BK[bb]])
                S.add("act", lambda e, b=b, bb=bb: e.activation(out=Vpad[:, 1 + b, 0, 0:64], in_=bank(bb)[:, 0:64], func=AF.Copy),
                      reads=[BK[bb]], writes=[VPB[1 + b]])
                S.add("act", lambda e, b=b, bb=bb: e.activation(out=Vpad[:, 1 + b, 1, 64:128], in_=bank(bb)[:, 64:128], func=AF.Copy),
                      reads=[BK[bb]], writes=[VPB[1 + b]])
        stages.append((pieces, fn_kv, ("kv",)))

        if not halo:
            def attn_unit(qb, u, uidx):
                first = (ti == 1 and qb == 0)
                mk = amask[:, 1 if first else 0, :]
                es = uidx % 2
                Ev = Eviews[es]
                EBx = EBS[es]
                denb = FT[uidx % 2]
                den = Ft[uidx % 2]
                for g in range(2):
                    d2 = (0, 1, 3)[rot_state["sc"] % 3]
                    rot_state["sc"] += 1
                    PSd = PS[d2]
                    bks = [BK[2 * d2], BK[2 * d2 + 1]]

                    def mm(e, g=g, PSd=PSd):
                        ins = None
                        for kb in range(2):
                            for ci in range(4):
                                cp = u * 4 + ci
                                ins = e.matmul(PSd[:, kb * 512 + ci * 128: kb * 512 + ci * 128 + 128],
                                               kT[g * 64:(g + 1) * 64, (qb + kb) * 128:(qb + kb + 1) * 128],
                                               act[g * 64:(g + 1) * 64, cp, qb * 128:(qb + 1) * 128],
                                               start=True, stop=True)
                        return ins
                    S.add("pe", mm, reads=[KTB[qb], KTB[qb + 1]] + A[u * 4:u * 4 + 4], writes=bks)
                    yield
                    S.add("act", lambda e, g=g, PSd=PSd: e.activation(out=Ev[:, g, :], in_=PSd[:, :], func=AF.Exp, scale=0.125),
                          reads=bks, writes=[EBx[g]])
                    yield
                    S.add("dve", lambda e, g=g: e.tensor_tensor(out=Ev[:, g, :], in0=Ev[:, g, :], in1=mk, op=ALU.mult),
                          reads=[EBx[g], CONST], writes=[EBx[g]])
                    yield

                def pv(e):
                    ins = None
                    n = 0
                    for g in range(2):
                        for kb in range(2):
                            ins = e.matmul(bank(4)[:, :], Vpad[:, qb + kb, g, :], Ev[:, g, kb * 512:(kb + 1) * 512],
                                           start=(n == 0), stop=(n == 3))
                            n += 1
                    n = 0
                    for g in range(2):
                        for kb in range(2):
                            ins = e.matmul(bank(5)[:, :], onespad[:, g, :], Ev[:, g, kb * 512:(kb + 1) * 512],
                                           start=(n == 0), stop=(n == 3))
                            n += 1
                    return ins
                S.add("pe", pv, reads=[EBx[0], EBx[1], VPB[qb], VPB[qb + 1], "onespad"], writes=[BK[4], BK[5]])
                yield
                den3 = den[:, :].rearrange("p (c q) -> p c q", c=4)
                S.add("dve", lambda e: e.tensor_tensor(
                    out=den3, in0=bank(5)[:, :].rearrange("p (c q) -> p c q", c=4),
                    in1=esink[:, u * 4:u * 4 + 4].unsqueeze(2).to_broadcast([128, 4, 128]), op=ALU.add),
                    reads=[BK[5], "esink"], writes=[denb])
                yield
                S.add("act", lambda e: e.activation(out=den[:, :], in_=den[:, :], func=AF.Ln), reads=[denb], writes=[denb])
                yield
                S.add("act", lambda e: e.activation(out=den[:, :], in_=den[:, :], func=AF.Exp, scale=-1.0), reads=[denb], writes=[denb])
                yield
                S.add("dve", lambda e: e.tensor_tensor(
                    out=act[:, 8 + u * 4:8 + u * 4 + 4, qb * 128:(qb + 1) * 128],
                    in0=bank(4)[:, :].rearrange("p (c q) -> p c q", c=4), in1=den3, op=ALU.mult),
                    reads=[BK[4], denb], writes=A[8 + u * 4:8 + u * 4 + 4])
                yield

            def st_attn(_):
                units = []
                n = 0
                for qb in range(nblk):
                    for u in range(2):
                        units.append(attn_unit(qb, u, n))
                        n += 1
                pipeline(units, depth=2, offset=4)
            stages.append((None, st_attn))

        def st_shift(_):
            S.add("dve", lambda e: e.tensor_copy(out=kT[:, 0:128], in_=kT[:, nblk * 128:(nblk + 1) * 128]),
                  reads=[KTB[nblk]], writes=[KTB[0]])
            S.add("dve", lambda e: e.tensor_copy(out=Vpad[:, 0, :, :], in_=Vpad[:, nblk, :, :]),
                  reads=[VPB[nblk]], writes=[VPB[0]])
        stages.append((None, st_shift))

        for half in range(2):
            pieces = [((KC, 512, 0, 512), win3[:, :, OGI + half * 512:OGI + (half + 1) * 512])]

            def fn_gi(slot, half=half):
                w3 = wview3(slot, KC, 512)
                for b in range(nblk):
                    bb = (b % 2)

                    def mm(e, b=b, bb=bb):
                        ins = None
                        for kc in range(KC):
                            ins = e.matmul(bank(bb)[:, :], hbf[:, kc, b * 128:(b + 1) * 128], w3[:, kc, :],
                                           start=(kc == 0), stop=(kc == KC - 1))
                        return ins
                    S.add("pe", mm, reads=[WS[slot], Hall], writes=[BK[bb]])
                    S.add("act", lambda e, b=b, bb=bb: e.activation(out=vhv[:, b, half * 512:(half + 1) * 512], in_=bank(bb)[:, :], func=AF.Copy),
                          reads=[BK[bb]], writes=[VH[b]])
            stages.append((pieces, fn_gi, ("gi", half)))

        for hd in range(8):
            pieces = [((KC, 384, 0, 128), win3[:, :, OGF + hd * 128:OGF + (hd + 1) * 128]),
                      ((KC, 384, 128, 256), win3[:, :, OGQ + hd * 128:OGQ + (hd + 1) * 128]),
                      ((KC, 384, 256, 384), win3[:, :, OGO + hd * 128:OGO + (hd + 1) * 128])]

            def fn_hg(slot, hd=hd):
                par = hd % 2
                R_ = HGSET[par]
                KtTx, KTTx, Qhx, QHx, Ktx, KTx, ATmx, ATBx = (R_["KtT"], R_["KTT"], R_["Qh"], R_["QH"], R_["Kt"],
                                                              R_["KT"], R_["ATm"], R_["ATB"])
                w3 = wview3(slot, KC, 384)
                tf, tl, tb, teb, tq = HTSET[par]
                tenb = tl
                ob = 3 - par
                ab = 4 + par
                ub = 6 + par
                pstr_ = pstrs[par]
                Khx = Kh[par]
                KHx = Buf(Khx[:], [("khat", par)])

                def pbank():
                    b = rot_state["pj"] % 2
                    rot_state["pj"] += 1
                    return b
                pb = pbank()
                proj(slot, w3, 0, pb)
                yield
                if not halo:
                    pb2 = pbank()
                    proj(slot, w3, 128, pb2)
                    yield
                S.add("act", lambda e: e.activation(out=tf.ap[:, 0:T], in_=bank(pb)[:, 0:T], func=AF.Sigmoid), reads=[BK[pb]], writes=[tf])
                yield
                if not halo:
                    S.add("act", lambda e: e.activation(out=tq.ap[:, 0:T], in_=bank(pb2)[:, 0:T], func=AF.Sigmoid), reads=[BK[pb2]], writes=[tq])
                    yield
                    S.add("dve", lambda e: e.tensor_tensor(out=tq.ap[:, 0:T], in0=tq.ap[:, 0:T], in1=bank(pb2)[:, 0:T], op=ALU.mult),
                          reads=[tq, BK[pb2]], writes=[tq])
                    yield
                    pb3 = pbank()
                    proj(slot, w3, 256, pb3)
                    yield
                    S.add("act", lambda e: e.activation(out=tb.ap[:, 0:T], in_=bank(pb3)[:, 0:T], func=AF.Sigmoid), reads=[BK[pb3]], writes=[tb])
                    yield
                    S.add("dve", lambda e: e.tensor_tensor(out=act[:, 24 + hd, 0:T], in0=tb.ap[:, 0:T], in1=bank(pb3)[:, 0:T], op=ALU.mult),
                          reads=[tb, BK[pb3]], writes=[A[24 + hd]])
                    yield
                S.add("dve", lambda e: e.tensor_scalar(out=tf.ap[:, 0:T], in0=tf.ap[:, 0:T], scalar1=lbv[:, 1, hd:hd + 1],
                                                       scalar2=lbv[:, 0, hd:hd + 1], op0=ALU.mult, op1=ALU.add),
                      reads=[tf, LBV], writes=[tf])
                yield
                S.add("act", lambda e: e.activation(out=tl.ap[:, 0:T], in_=tf.ap[:, 0:T], func=AF.Ln), reads=[tf], writes=[tl])
                yield
                S.add("dve", lambda e: e.tensor_tensor_scan(out=tb.ap[:, 0:T], data0=rmask[:, 0:T], data1=tl.ap[:, 0:T], initial=0.0,
                                                            op0=ALU.mult, op1=ALU.add), reads=[tl, CONST], writes=[tb])
                yield
                S.add("act", lambda e: e.activation(out=teb.ap[:, 0:T], in_=tb.ap[:, 0:T], func=AF.Exp), reads=[tb], writes=[teb])
                yield
                S.add("act", lambda e: e.activation(out=tenb.ap[:, 0:T], in_=tb.ap[:, 0:T], func=AF.Exp, scale=-1.0), reads=[tb], writes=[tenb])
                yield
                S.add("dve", lambda e: e.tensor_scalar(out=tf.ap[:, 0:T], in0=tf.ap[:, 0:T], scalar1=-1.0, scalar2=1.0,
                                                       op0=ALU.mult, op1=ALU.add), reads=[tf], writes=[tf])
                yield
                S.add("dve", lambda e: e.tensor_tensor(out=Ktx[:, 0:T], in0=tf.ap[:, 0:T], in1=tenb.ap[:, 0:T], op=ALU.mult),
                      reads=[tf, tenb], writes=[KTx])
                yield
                if not halo:
                    S.add("dve", lambda e: e.tensor_tensor(out=Qhx[:, 0:T], in0=tq.ap[:, 0:T], in1=teb.ap[:, 0:T], op=ALU.mult),
                          reads=[tq, teb], writes=[QHx])
                    yield
                nch = T // 64
                S.add("dve", lambda e: e.tensor_tensor(
                    out=Khx[:, 0:T].rearrange("p (c t) -> p c t", t=64), in0=Ktx[:, 0:T].rearrange("p (c t) -> p c t", t=64),
                    in1=teb.ap[:, 0:T].rearrange("p (c t) -> p c t", t=64)[:, :, 63:64].to_broadcast([128, nch, 64]), op=ALU.mult),
                    reads=[KTx, teb], writes=[KHx])
                yield
                for b in range(nblk):
                    S.add("pe", lambda e, b=b: e.transpose(pstr_[:, b * 128:(b + 1) * 128], Khx[:, b * 128:(b + 1) * 128], ident_bf),
                          reads=[KHx, CONST], writes=[BK[ab]])
                yield
                S.add("act", lambda e: e.activation(out=KtTx[:, 0:nblk, :], in_=pstr_[:, 0:nblk * 128].rearrange("p (b d) -> p b d", b=nblk),
                                                    func=AF.Copy), reads=[BK[ab]], writes=KTTx[0:nblk])
                yield
                nchk = 2 * nblk

                def u_mm(c):
                    b_, c2_ = divmod(c, 2)
                    ubc = ub if c % 2 == 0 else ab
                    S.add("pe", lambda e: e.matmul(bank(ubc)[:, 0:128], KtTx[c2_ * 64:(c2_ + 1) * 64, b_, :],
                                                   vhv[c2_ * 64:(c2_ + 1) * 64, b_, hd * 128:(hd + 1) * 128],
                                                   start=True, stop=True), reads=[KTTx[b_], VH[b_]], writes=[BK[ubc]])

                def s_in(c):
                    if c == 0:
                        return Sbf[:, hd, :], SBF[hd]
                    return Sring[:, par, (c - 1) % 2, :], SRB[par][(c - 1) % 2]

                for c in range(min(1, nchk)):
                    u_mm(c)
                yield
                for c in range(nchk):
                    b = c // 2
                    col = c * 64
                    ubc = ub if c % 2 == 0 else ab
                    if c % 2 == 1 and c + 1 < nchk:
                        u_mm(c + 1)
                        yield
                    if c % 2 == 0 and not halo:
                        S.add("pe", lambda e, b=b: e.matmul(bank(ab)[:, 0:128], Ktx[:, b * 128:(b + 1) * 128], Qhx[:, b * 128:(b + 1) * 128],
                                                            start=True, stop=True), reads=[KTx, QHx], writes=[BK[ab]])
                        yield
                        S.add("dve", lambda e, b=b: e.tensor_tensor(out=ATmx[:, b, :], in0=bank(ab)[:, 0:128], in1=hgmask_bf, op=ALU.mult),
                              reads=[BK[ab], CONST], writes=[ATBx[b]])
                        yield
                    if c % 2 == 0 and c + 1 < nchk:
                        u_mm(c + 1)
                        yield
                        S.add("pe", lambda e, b=b: e.matmul(bank(ob)[:, b * 128:(b + 1) * 128], vhv[:, b, hd * 128:(hd + 1) * 128], ATmx[:, b, :],
                                                            start=True, stop=False), reads=[VH[b], ATBx[b]], writes=[BK[ob]])
                        yield
                    if not halo:
                        sap, sbuf_ = s_in(c)
                        S.add("pe", lambda e, col=col, c=c, sap=sap: e.matmul(bank(ob)[:, col:col + 64], sap, Qhx[:, col:col + 64],
                                                                             start=False, stop=(c % 2 == 1)),
                              reads=[sbuf_, QHx], writes=[BK[ob]])
                        yield
                    if c == nchk - 1:
                        dap, dbuf = Sbf[:, hd, :], SBF[hd]
                    else:
                        dap, dbuf = Sring[:, par, c % 2, :], SRB[par][c % 2]
                    S.add("dve", lambda e, col=col, ubc=ubc, dap=dap: e.scalar_tensor_tensor(
                        out=dap, in0=S32[:, hd, :], scalar=teb.ap[:, col + 63:col + 64], in1=bank(ubc)[:, 0:128],
                        op0=ALU.mult, op1=ALU.add), reads=[BK[ubc], SB32[hd], teb], writes=[dbuf])
                    yield
                    S.add("dve", lambda e, col=col, ubc=ubc: e.scalar_tensor_tensor(
                        out=S32[:, hd, :], in0=S32[:, hd, :], scalar=teb.ap[:, col + 63:col + 64], in1=bank(ubc)[:, 0:128],
                        op0=ALU.mult, op1=ALU.add), reads=[BK[ubc], SB32[hd], teb], writes=[SB32[hd]])
                    yield
                if not halo:
                    o32, osq, lnv = tf, tl, tb
                    S.add("act", lambda e: e.activation(out=o32.ap[:, 0:T], in_=bank(ob)[:, 0:T], func=AF.Copy), reads=[BK[ob]], writes=[o32])
                    yield
                    S.add("act", lambda e: e.activation(out=osq.ap[:, 0:T], in_=bank(ob)[:, 0:T], func=AF.Square), reads=[BK[ob]], writes=[osq])
                    yield
                    pb4 = pbank()
                    S.add("pe", lambda e: e.matmul(bank(pb4)[:, 0:T], ones32, osq.ap[:, 0:T], start=True, stop=True),
                          reads=[osq, CONST], writes=[BK[pb4]])
                    yield
                    S.add("act", lambda e: e.activation(out=lnv.ap[:, 0:T], in_=bank(pb4)[:, 0:T], func=AF.Ln, scale=1.0 / 128, bias=cvec[:, 0:1]),
                          reads=[BK[pb4], CV], writes=[lnv])
                    yield
                    S.add("act", lambda e: e.activation(out=lnv.ap[:, 0:T], in_=lnv.ap[:, 0:T], func=AF.Exp, scale=-0.5), reads=[lnv], writes=[lnv])
                    yield
                    S.add("dve", lambda e: e.scalar_tensor_tensor(out=o32.ap[:, 0:T], in0=o32.ap[:, 0:T], scalar=hgg[:, 0:1], in1=lnv.ap[:, 0:T],
                                                                  op0=ALU.mult, op1=ALU.mult), reads=[o32, lnv, CONST], writes=[o32])
                    yield
                    S.add("dve", lambda e: e.tensor_tensor(out=act[:, 24 + hd, 0:T], in0=o32.ap[:, 0:T], in1=act[:, 24 + hd, 0:T], op=ALU.mult),
                          reads=[o32, A[24 + hd]], writes=[A[24 + hd]])
                    yield
            stages.append((pieces, fn_hg, ("hg", hd), "pipe_hg"))

        if halo:
            return

        wA4 = wA.rearrange("(g c d) n -> d g c n", g=2, c=8)
        wR3 = wR.rearrange("(c p) n -> p c n", p=128)

        def merged_slot(m):
            return m if m < 8 else 16 + (m - 8)

        for mp in range(8):
            def ar_dma(base, mp=mp):
                w3 = view3(base, 16, 256)
                res = []
                for g in range(2):
                    res.append((w3[g * 64:(g + 1) * 64, 0:8, :], wA4[:, g, :, mp * 256:(mp + 1) * 256]))
                res.append((w3[:, 8:16, :], wR3[:, :, mp * 256:(mp + 1) * 256]))
                return res
            arslot = {}

            def fn_ar(slot, arslot=arslot):
                arslot["s"] = slot
            stages.append((("raw", ar_dma), fn_ar, ("ar", mp)))
            piecesBR = [((KC, 512, 0, 256), win3[:, :, OBR + mp * 256:OBR + mp * 256 + 256]),
                        ((KC, 512, 256, 512), win3[:, :, OBR + D + mp * 256:OBR + D + mp * 256 + 256])]

            def fn_br(slot, mp=mp, arslot=arslot):
                sA = arslot["s"]
                wa3 = wview3(sA, 16, 256)
                w3 = wview3(slot, KC, 512)
                for mi in range(2):
                    m = mp * 2 + mi
                    off = mi * 128
                    b0 = (mi % 2) * 4
                    pAb, pRb, gAb, gRb = b0, b0 + 1, b0 + 2, b0 + 3

                    def mmA(e, off=off, pAb=pAb):
                        ins = None
                        for c in range(8):
                            ins = e.matmul(bank(pAb)[:, 0:T], wa3[:, c, off:off + 128], act[:, 8 + c, 0:T], start=(c == 0), stop=(c == 7))
                        return ins
                    S.add("pe", mmA, reads=[WS[sA]] + A[8:16], writes=[BK[pAb]])

                    def mmR(e, off=off, pRb=pRb):
                        ins = None
                        for c in range(8):
                            ins = e.matmul(bank(pRb)[:, 0:T], wa3[:, 8 + c, off:off + 128], act[:, 24 + c, 0:T], start=(c == 0), stop=(c == 7))
                        return ins
                    S.add("pe", mmR, reads=[WS[sA]] + A[24:32], writes=[BK[pRb]])
                    proj(slot, w3, mi * 128, gAb)
                    proj(slot, w3, 256 + mi * 128, gRb)
                    S.add("act", lambda e, gAb=gAb: e.activation(out=Ft[0][:, 0:T], in_=bank(gAb)[:, 0:T], func=AF.Sigmoid), reads=[BK[gAb]], writes=[FT[0]])
                    S.add("act", lambda e, gRb=gRb: e.activation(out=Ft[1][:, 0:T], in_=bank(gRb)[:, 0:T], func=AF.Sigmoid), reads=[BK[gRb]], writes=[FT[1]])
                    S.add("dve", lambda e, pAb=pAb: e.tensor_tensor(out=Ft[0][:, 0:T], in0=Ft[0][:, 0:T], in1=bank(pAb)[:, 0:T], op=ALU.mult),
                          reads=[FT[0], BK[pAb]], writes=[FT[0]])
                    S.add("dve", lambda e, pRb=pRb: e.tensor_tensor(out=Ft[1][:, 0:T], in0=Ft[1][:, 0:T], in1=bank(pRb)[:, 0:T], op=ALU.mult),
                          reads=[FT[1], BK[pRb]], writes=[FT[1]])
                    ms = merged_slot(m)
                    S.add("dve", lambda e, ms=ms: e.tensor_tensor(out=act[:, ms, 0:T], in0=Ft[0][:, 0:T], in1=Ft[1][:, 0:T], op=ALU.add),
                          reads=[FT[0], FT[1]], writes=[A[ms]])
            stages.append((piecesBR, fn_br, ("br", mp)))

        wO3 = wO.rearrange("(c p) n -> p c n", p=128)
        MERG = [A[merged_slot(m)] for m in range(16)]
        for mq in range(4):
            pieces = [((KC, 512, 0, 512), wO3[:, :, mq * 512:(mq + 1) * 512])]

            def fn_o(slot, mq=mq):
                w3 = wview3(slot, KC, 512)
                for mi in range(4):
                    m = mq * 4 + mi
                    bb = mi % 4

                    def mm(e, mi=mi, bb=bb):
                        ins = None
                        for kc in range(KC):
                            ins = e.matmul(bank(bb)[:, 0:T], w3[:, kc, mi * 128:(mi + 1) * 128], act[:, merged_slot(kc), 0:T],
                                           start=(kc == 0), stop=(kc == KC - 1))
                        return ins
                    S.add("pe", mm, reads=[WS[slot]] + MERG, writes=[BK[bb]])
                    S.add("dve", lambda e, m=m, bb=bb: e.tensor_tensor(out=x32[:, m, 0:T], in0=x32[:, m, 0:T], in1=bank(bb)[:, 0:T], op=ALU.add),
                          reads=[X[m], BK[bb]], writes=[X[m]])
            stages.append((pieces, fn_o, ("wo", mq)))

        ffn(1)

    tiles = [(0, 0, HALO, True)] + [(1 + i, HALO + 512 * i, 512, False) for i in range(8)]
    for (ti, t0, T, halo) in tiles[:n_tiles]:
        tile_prog(ti, t0, T, halo)

    def pieces_to_list(pieces, base):
        if isinstance(pieces, tuple) and pieces[0] == "raw":
            return pieces[1](base)
        return [(view3(base, a, b)[:, :, c0:c1], src) for (a, b, c0, c1), src in pieces]

    scratch = {}
    ncv = 0
    for st in stages:
        if st[0] is None:
            continue
        sid = st[2]
        if sid in scratch:
            continue
        scr = nc.dram_tensor("scr_" + "_".join(str(v) for v in sid), [128, SLOT_ELEMS], BF16).ap()
        scratch[sid] = scr
        tok = ncv % 8
        for dst, src in pieces_to_list(st[0], scr):
            S.add("pool", lambda e, dst=dst, src=src: e.dma_start(out=dst, in_=src),
                  writes=[("scr", sid), ("cvtok", tok)], dma=("cv", tok))
        ncv += 1

    wstages = [i for i, st in enumerate(stages) if st[0] is not None]
    slot_of = {si: n % NSLOT for n, si in enumerate(wstages)}
    planned = [0]

    def stage_elems(pieces):
        if isinstance(pieces, tuple) and pieces[0] == "raw":
            return 16 * 256
        return max(a * b for (a, b, c0, c1), src in pieces)

    def plan_dma(upto_n):
        while planned[0] < len(wstages) and planned[0] <= upto_n:
            si = wstages[planned[0]]
            slot = slot_of[si]
            sid = stages[si][2]
            ne = stage_elems(stages[si][0])
            S.add("sp", lambda e, slot=slot, sid=sid, ne=ne: e.dma_start(out=wsl[slot][:, 0:ne], in_=scratch[sid][:, 0:ne]),
                  reads=[("scr", sid)], writes=[WS[slot]], dma=("wsem", slot))
            planned[0] += 1

    widx = {si: n for n, si in enumerate(wstages)}
    si = 0
    while si < len(stages):
        st = stages[si]
        if len(st) > 3:
            grp = [si]
            while grp[-1] + 1 < len(stages) and len(stages[grp[-1] + 1]) > 3 and stages[grp[-1] + 1][3] == st[3]:
                grp.append(grp[-1] + 1)

            def mk(sj):
                def g():
                    plan_dma(widx[sj] + NSLOT - 2)
                    yield from stages[sj][1](slot_of[sj])
                return g()
            pipeline([mk(sj) for sj in grp], depth=2, offset=24)
            si = grp[-1] + 1
            continue
        if st[0] is not None:
            plan_dma(widx[si] + NSLOT - 2)
            st[1](slot_of[si])
        else:
            st[1](None)
        si += 1
    S.add("sp", lambda e: e.nop(), reads=[("outdram", m) for m in range(KC)])
    S.emit()
    return nc


_CACHE = {}


def _consts():
    c = np.zeros((128, 5, 128), np.float32)
    c[:, 0, :] = np.eye(128, dtype=np.float32)
    c[:, 1, :] = 1.0
    for blk in range(2):
        for i in range(32):
            c[blk * 64 + i + 32, 2, blk * 64 + i] = -1.0
            c[blk * 64 + i, 2, blk * 64 + i + 32] = 1.0
    s = np.arange(128)[:, None]
    t = np.arange(128)[None, :]
    c[:, 3, :] = ((s // 64 == t // 64) & (s <= t)).astype(np.float32)
    c[:, 4, :] = (s // 64 == t // 64).astype(np.float32)
    rmask = np.ones((128, 512), np.float32)
    rmask[:, ::64] = 0.0
    inv_freq = (np.float32(10000.0) ** (-np.arange(32, dtype=np.float32) * np.float32(2.0) / np.float32(64))).astype(np.float32)
    ropec = np.tile(inv_freq, 4).reshape(128, 1).astype(np.float32)
    j = np.arange(128)[:, None]
    i = np.arange(128)[None, :]
    prev = (j > i).astype(np.float32)
    cur = (j <= i).astype(np.float32)
    am = np.zeros((128, 2, 2, 4, 128), np.float32)
    am[:, 0, 0] = prev[:, None, :]
    am[:, 0, 1] = cur[:, None, :]
    am[:, 1, 0] = prev[:, None, :]
    am[:, 1, 1] = cur[:, None, :]
    return c, rmask, ropec, am.reshape(128, 2, 1024)


def kernel(x, positions, lb_table, ffn1_norm, ffn1_w_gu, ffn1_w_down, mix_norm, w_in, q_norm, k_norm,
           sinks, hg_out_norm, w_attn_branch, w_hg_branch, w_out, ffn2_norm, ffn2_w_gu, ffn2_w_down,
           _n_tiles=9, _cores=None):
    f = lambda a: np.ascontiguousarray(np.asarray(a), dtype=np.float32)
    x = f(x)
    positions = np.ascontiguousarray(np.asarray(positions), dtype=np.int32)
    key = _n_tiles
    if key not in _CACHE:
        _CACHE[key] = build(_n_tiles)
    nc = _CACHE[key]
    cm, rmask, ropec, am = _consts()
    gains = np.stack([f(ffn1_norm)[0].reshape(KC, 128).T, f(mix_norm)[0].reshape(KC, 128).T,
                      f(ffn2_norm)[0].reshape(KC, 128).T], axis=1)
    qkg = np.stack([np.tile(f(q_norm)[0], 2), np.tile(f(k_norm)[0], 2)], axis=1)
    hgg = f(hg_out_norm)[0].reshape(128, 1)
    lbt = f(lb_table).reshape(2, 8, 128).transpose(2, 0, 1)
    shared = {
        "wgu1": f(ffn1_w_gu)[0], "wgu2": f(ffn2_w_gu)[0], "wd1": f(ffn1_w_down)[0], "wd2": f(ffn2_w_down)[0],
        "win": f(w_in)[0], "wA": f(w_attn_branch)[0], "wR": f(w_hg_branch)[0], "wO": f(w_out)[0],
        "gains": np.ascontiguousarray(gains), "qkg": np.ascontiguousarray(qkg), "hgg": np.ascontiguousarray(hgg),
        "lbt": np.ascontiguousarray(lbt), "sinks": f(sinks).reshape(1, 16),
        "cmat": cm, "rmask": rmask, "ropec": ropec,
    }
    cores = list(range(8)) if _cores is None else _cores
    in_maps = []
    for cid in cores:
        b = cid // 4
        s0 = (cid % 4) * SEG
        xt = np.zeros((D, NT), np.float32)
        ps = np.zeros((1, NT), np.int32)
        if s0 > 0:
            xt[:, :] = x[b, s0 - HALO:s0 + SEG, :].T
            ps[0, :] = positions[b, s0 - HALO:s0 + SEG]
        else:
            xt[:, HALO:] = x[b, 0:SEG, :].T
            ps[0, HALO:] = positions[b, 0:SEG]
        am_c = am.copy()
        if s0 == 0:
            am_c[:, 1, 0:512] = 0.0
        m = dict(shared)
        m["xT"] = xt
        m["pos"] = ps
        m["amask"] = am_c
        in_maps.append(m)
    res = run_bass_kernel_spmd(nc, in_maps, core_ids=list(range(len(cores))))
    out = np.zeros((2, SEQ, D), np.float32)
    for k, cid in enumerate(cores):
        b = cid // 4
        s0 = (cid % 4) * SEG
        out[b, s0:s0 + SEG, :] = res.results[k]["outT"].T
    return out
```

```python
import numpy as np
import concourse.bass as bass
import concourse.mybir as mybir
from concourse.bass_utils import run_bass_kernel_spmd

F32 = mybir.dt.float32
BF16 = mybir.dt.bfloat16
I32 = mybir.dt.int32
AF = mybir.ActivationFunctionType
ALU = mybir.AluOpType

D = 2048
DFF = 5632
KC = D // 128
FC = DFF // 128
SEQ = 16384
SEG = 4096
HALO = 128
NT = SEG + HALO
EPS = 1e-6
TMAX = 512
NSLOT = 4
SLOT_ELEMS = 8192
OQ, OK_, OV, OGQ, OGF, OGI, OGO, OBR = 0, 1024, 1152, 1280, 2304, 3328, 4352, 5376
TWO_PI_HI = 6.28125
TWO_PI_LO = 2.0 * np.pi - 6.28125
PI_CLAMP = 3.1415925


def pipeline(gens, depth=2, offset=0):
    it = iter(gens)
    active = []
    while True:
        while len(active) < depth:
            g = next(it, None)
            if g is None:
                break
            if active:
                for _ in range(offset):
                    for a in list(active):
                        try:
                            next(a)
                        except StopIteration:
                            active.remove(a)
            active.append(g)
        if not active:
            break
        for a in list(active):
            try:
                next(a)
            except StopIteration:
                active.remove(a)


class Buf:
    __slots__ = ("ap", "keys")

    def __init__(self, ap, keys):
        self.ap = ap
        self.keys = tuple(keys)


def _keys(items):
    out = []
    for it in items:
        if isinstance(it, Buf):
            out.extend(it.keys)
        else:
            out.append(it)
    return out


class Sched:
    ENGS = ("pe", "act", "dve", "pool", "sp")

    def __init__(self, nc):
        self.nc = nc
        self.ops = {e: [] for e in self.ENGS}
        self.lastw = {}
        self.readers = {}
        self.dma_cnt = {}

    def add(self, eng, fn, reads=(), writes=(), dma=None):
        rk = _keys(reads)
        wk = _keys(writes)
        deps = {}

        def need(ev):
            k = (ev[0], ev[1])
            if deps.get(k, -1) < ev[2]:
                deps[k] = ev[2]

        for k in rk:
            ev = self.lastw.get(k)
            if ev is not None:
                need(ev)
        for k in wk:
            ev = self.lastw.get(k)
            if ev is not None:
                need(ev)
            rd = self.readers.get(k)
            if rd:
                for kk, vv in rd.items():
                    need((kk[0], kk[1], vv))
        if eng == "pe":
            deps.pop(("c", "pe"), None)
        idx = len(self.ops[eng])
        if dma is not None:
            c = self.dma_cnt.get(dma, 0) + 1
            self.dma_cnt[dma] = c
            ev = ("d", dma, 16 * c)
        else:
            ev = ("c", eng, idx)
        for (ty, who), v in deps.items():
            if ty == "c":
                self.ops[who][v][3] = True
        self.ops[eng].append([fn, deps, dma, False])
        for k in wk:
            self.lastw[k] = ev
            self.readers[k] = {}
        wset = set(wk)
        for k in rk:
            if k in wset:
                continue
            rd = self.readers.setdefault(k, {})
            kk = (ev[0], ev[1])
            if rd.get(kk, -1) < ev[2]:
                rd[kk] = ev[2]
        return ev

    def emit(self):
        nc = self.nc
        sems = {e: nc.alloc_semaphore("s_" + e) for e in ("pe", "act", "dve", "pool")}
        dsem = {k: nc.alloc_semaphore("d_%d" % i) for i, k in enumerate(self.dma_cnt)}
        cum = {}
        for e, ops in self.ops.items():
            c = 0
            arr = []
            for o in ops:
                if o[3] and o[2] is None:
                    c += 1
                arr.append(c)
            cum[e] = arr
        ops_all = self.ops

        def run(e, engine):
            seen = {}
            for fn, deps, dma, sig in ops_all[e]:
                for (ty, who), v in deps.items():
                    if ty == "c":
                        s = sems[who]
                        val = cum[who][v]
                    else:
                        s = dsem[who]
                        val = v
                    if seen.get((ty, who), 0) >= val:
                        continue
                    seen[(ty, who)] = val
                    engine.wait_ge(s, val)
                ins = fn(engine)
                if dma is not None:
                    ins.then_inc(dsem[dma], 16)
                elif sig:
                    ins.then_inc(sems[e], 1)

        with nc.Block() as block:
            @block.tensor
            def _(eng):
                run("pe", eng)

            @block.scalar
            def _(eng):
                run("act", eng)

            @block.vector
            def _(eng):
                run("dve", eng)

            @block.gpsimd
            def _(eng):
                run("pool", eng)

            @block.sync
            def _(eng):
                run("sp", eng)


def build(n_tiles=9):
    nc = bass.Bass("TRN2", target_bir_lowering=False)
    S = Sched(nc)

    def din(name, shape, dt=F32):
        return nc.dram_tensor(name, list(shape), dt, kind="ExternalInput").ap()

    xT = din("xT", [D, NT])
    pos_d = din("pos", [1, NT], I32)
    wgu = [din("wgu1", [D, 2 * DFF]), din("wgu2", [D, 2 * DFF])]
    wdn = [din("wd1", [DFF, D]), din("wd2", [DFF, D])]
    win = din("win", [D, 9472])
    wA = din("wA", [1024, D])
    wR = din("wR", [1024, D])
    wO = din("wO", [D, D])
    gains_d = din("gains", [128, 3, KC])
    qkg_d = din("qkg", [128, 2])
    hgg_d = din("hgg", [128, 1])
    lbt_d = din("lbt", [128, 2, 8])
    sinks_d = din("sinks", [1, 16])
    cmat_d = din("cmat", [128, 5, 128])
    amask_d = din("amask", [128, 2, 1024])
    rmask_d = din("rmask", [128, 512])
    ropec_d = din("ropec", [128, 1])
    outT = nc.dram_tensor("outT", [D, SEG], F32, kind="ExternalOutput").ap()

    def sb(name, shape, dt):
        return nc.alloc_sbuf_tensor("sb_" + name, shape, dt)
    x32 = sb("x32", [128, KC, TMAX], F32)
    hbf = sb("hbf", [128, KC, TMAX], BF16)
    act = sb("act", [128, FC, TMAX], BF16)
    wsl = [sb("wsl%d" % i, [128, SLOT_ELEMS], BF16) for i in range(NSLOT)]
    Ft = [sb("ftmp%d" % i, [128, TMAX], F32) for i in range(3)]
    x32h = sb("x32h", [128, KC, HALO], F32)
    cm32 = sb("cm32", [128, 5, 128], F32)
    cmbf = sb("cmbf", [128, 5, 128], BF16)
    amask = sb("amask", [128, 2, 1024], BF16)
    rmask = sb("rmask", [128, 512], F32)
    cosT = sb("cosT", [128, TMAX], F32)
    sinT = sb("sinT", [128, TMAX], F32)
    posi = sb("posi", [128, TMAX], I32)
    gains = sb("gains", [128, 3, KC], F32)
    qkg = sb("qkg", [128, 2], F32)
    hgg = sb("hgg", [128, 1], F32)
    lbt = sb("lbt", [128, 2, 8], F32)
    lbv = sb("lbv", [128, 2, 8], F32)
    esink = sb("esink", [128, 8], F32)
    ropec = sb("ropec", [128, 1], F32)
    cvec = sb("cvec", [128, 4], F32)
    kT = sb("kT", [128, 5 * 128], BF16)
    Vpad = sb("Vpad", [128, 5, 2, 128], BF16)
    onespad = sb("onespad", [128, 2, 128], BF16)
    S32 = sb("S32", [128, 8, 128], F32)
    Sbf = sb("Sbf", [128, 8, 128], BF16)
    KtT = sb("KtT", [128, 4, 128], BF16)
    Qh = sb("Qh", [128, TMAX], BF16)
    Kt = sb("Kt", [128, TMAX], BF16)
    ATm = sb("ATm", [128, 4, 128], BF16)
    Sring = sb("Sring", [128, 2, 2, 128], BF16)
    KtT2 = sb("KtT2", [128, 4, 128], BF16)
    Qh2 = sb("Qh2", [128, TMAX], BF16)
    Kt2 = sb("Kt2", [128, TMAX], BF16)
    ATm2 = sb("ATm2", [128, 4, 128], BF16)
    Kh = [sb("Khat0", [128, TMAX], BF16), sb("Khat1", [128, TMAX], BF16)]

    PS = [nc.alloc_psum_tensor("psd%d" % i, [128, 1024], F32) for i in range(4)]

    def bank(b):
        return PS[b // 2][:, (b % 2) * 512:(b % 2) * 512 + 512]

    BK = [Buf(bank(b), [("ps", b)]) for b in range(8)]

    X = [Buf(x32[:, c, :], [("x", c)]) for c in range(KC)]
    H = [Buf(hbf[:, c, :], [("h", c)]) for c in range(KC)]
    Hall = Buf(None, [("h", c) for c in range(KC)])
    A = [Buf(act[:, j, :], [("a", j)]) for j in range(FC)]
    WS = [Buf(wsl[i][:], [("w", i)]) for i in range(NSLOT)]
    FT = [Buf(Ft[i][:], [("f", i)]) for i in range(3)]
    XH = [Buf(x32h[:, c, :], [("xh", c)]) for c in range(KC)]
    CONST = Buf(None, ["const"])
    COS = Buf(cosT[:], ["cos"])
    SIN = Buf(sinT[:], ["sin"])
    POSI = Buf(posi[:], ["posi"])
    KTB = [Buf(kT[:, b * 128:(b + 1) * 128], [("kT", b)]) for b in range(5)]
    VPB = [Buf(Vpad[:, b, :, :], [("vp", b)]) for b in range(5)]
    SB32 = [Buf(S32[:, h, :], [("s32", h)]) for h in range(8)]
    SBF = [Buf(Sbf[:, h, :], [("sbf", h)]) for h in range(8)]
    KTT = [Buf(KtT[:, b, :], [("ktt", b)]) for b in range(4)]
    QH = Buf(Qh[:], ["qh"])
    KT_ = Buf(Kt[:], ["kt"])
    ATB = [Buf(ATm[:, b, :], [("atm", b)]) for b in range(4)]
    SRB = [[Buf(Sring[:, p_, r_, :], [("sring", p_, r_)]) for r_ in range(2)] for p_ in range(2)]
    HGSET = [
        dict(KtT=KtT, KTT=KTT, Qh=Qh, QH=QH, Kt=Kt, KT=KT_, ATm=ATm, ATB=ATB),
        dict(KtT=KtT2, KTT=[Buf(KtT2[:, b, :], [("ktt2", b)]) for b in range(4)], Qh=Qh2, QH=Buf(Qh2[:], ["qh2"]),
             Kt=Kt2, KT=Buf(Kt2[:], ["kt2"]), ATm=ATm2, ATB=[Buf(ATm2[:, b, :], [("atm2", b)]) for b in range(4)]),
    ]

    def f32view(j):
        v = act[:, j:j + 2, :].bitcast(F32).rearrange("p a t -> p (a t)")
        return Buf(v, [("a", j), ("a", j + 1)])

    HT = [f32view(32 + 2 * i) for i in range(6)]
    HTSET = [HT[0:5], [HT[5]] + [f32view(2 * i) for i in range(4)]]
    Eviews = [act[:, 40:44, :].rearrange("p (g k) t -> p g (k t)", g=2),
              act[:, 36:40, :].rearrange("p (g k) t -> p g (k t)", g=2)]
    EBS = [[Buf(Eviews[0][:, g, :], [("a", 40 + 2 * g), ("a", 41 + 2 * g)]) for g in range(2)],
           [Buf(Eviews[1][:, g, :], [("a", 36 + 2 * g), ("a", 37 + 2 * g)]) for g in range(2)]]
    pstrs = [PS[2][:, 0:512].bitcast(BF16), PS[2][:, 512:1024].bitcast(BF16)]
    rot_state = {"sc": 0, "pj": 0}
    vhv = act[:, 16:24, :].rearrange("p (b a) t -> p b (a t)", b=4)
    VH = [Buf(vhv[:, b, :], [("a", 16 + 2 * b), ("a", 17 + 2 * b)]) for b in range(4)]
    pstr = PS[3][:, 512:1024].bitcast(BF16)

    ident_bf = cmbf[:, 0, :]
    ones_bf = cmbf[:, 1, :]
    hgmask_bf = cmbf[:, 3, :]
    rot32 = cm32[:, 2, :]
    hblk32 = cm32[:, 4, :]
    ones32 = cm32[:, 1, :]

    def ld(dst, src, eng="sp", sem="c0"):
        S.add(eng, lambda e: e.dma_start(out=dst, in_=src), writes=[CONST], dma=sem)

    ld(cm32[:], cmat_d)
    ld(rmask[:], rmask_d)
    ld(gains[:], gains_d)
    ld(qkg[:], qkg_d)
    ld(hgg[:], hgg_d)
    ld(lbt[:], lbt_d)
    ld(ropec[:], ropec_d)
    ld(esink[0:64, :], sinks_d[:, 0:8].partition_broadcast(64))
    ld(esink[64:128, :], sinks_d[:, 8:16].partition_broadcast(64))
    ld(cmbf[:], cmat_d, eng="pool", sem="c1")
    ld(amask[:], amask_d, eng="pool", sem="c1")
    C0 = Buf(None, ["const"])
    S.add("dve", lambda e: e.memset(cvec[:, 0:1], EPS), writes=["cv0"])
    S.add("dve", lambda e: e.memset(cvec[:, 1:2], float(np.pi / 2)), writes=["cv1"])
    S.add("dve", lambda e: e.memset(S32[:], 0.0), writes=SB32)
    S.add("dve", lambda e: e.memset(Sbf[:], 0.0), writes=SBF)
    S.add("dve", lambda e: e.memset(Vpad[:], 0.0), writes=VPB)
    S.add("dve", lambda e: e.memset(kT[:], 0.0), writes=KTB)
    S.add("dve", lambda e: e.memset(onespad[:], 0.0), writes=["onespad"])
    S.add("dve", lambda e: e.memset(onespad[:, 0, 0:64], 1.0), reads=[], writes=["onespad"])
    S.add("dve", lambda e: e.memset(onespad[:, 1, 64:128], 1.0), reads=[], writes=["onespad"])
    S.add("dve", lambda e: e.tensor_tensor(out=lbv[:, 1, :], in0=lbt[:, 0, :], in1=lbt[:, 1, :], op=ALU.subtract),
          reads=[C0], writes=["lbv1"])
    S.add("act", lambda e: e.activation(out=lbv[:, 0, :], in_=lbv[:, 1, :], func=AF.Sigmoid),
          reads=["lbv1"], writes=["lbv0"])
    S.add("dve", lambda e: e.tensor_scalar(out=lbv[:, 1, :], in0=lbv[:, 0, :], scalar1=-1.0, scalar2=1.0,
                                           op0=ALU.mult, op1=ALU.add), reads=["lbv0"], writes=["lbv1"])
    S.add("act", lambda e: e.activation(out=esink[:], in_=esink[:], func=AF.Exp), reads=[C0], writes=["esink"])
    LBV = Buf(None, ["lbv0", "lbv1"])
    CV = Buf(None, ["cv0", "cv1"])

    stages = []

    def view3(base, a, b):
        return base[:, 0:a * b].rearrange("p (a b) -> p a b", a=a)

    def wview3(slot, a, b):
        return view3(wsl[slot], a, b)

    def tile_prog(ti, t0, T, halo, part):
        nblk = T // 128
        xb = x32h if halo else x32
        XK = XH if halo else X
        o0 = t0 - HALO

        has_next = (ti + 1 < n_tiles)
        tn0 = t0 + T

        def load_x_chunk(c, c0, Tn, q="sp"):
            S.add(q, lambda e: e.dma_start(out=xb[:, c, 0:Tn], in_=xT[c * 128:(c + 1) * 128, c0:c0 + Tn]),
                  writes=[XK[c]], dma=("xld", c))

        def load_pos(c0, Tn, q="sp"):
            S.add(q, lambda e: e.dma_start(out=posi[:, 0:Tn], in_=pos_d[:, c0:c0 + Tn].partition_broadcast(128)),
                  writes=[POSI], dma="pld")

        if ti <= 1 and "A" in part:
            def st_load(_):
                for c in range(KC):
                    load_x_chunk(c, t0, T)
                if halo:
                    load_pos(t0, T)
            stages.append((None, st_load))

        def norm(gi):
            def fn(_):
                for c in range(KC):
                    r = c % 2
                    S.add("act", lambda e, c=c, r=r: e.activation(out=act[:, 42 + r, 0:T], in_=xb[:, c, 0:T], func=AF.Square),
                          reads=[XK[c]], writes=[A[42 + r]])
                    S.add("pe", lambda e, c=c, r=r: e.matmul(bank(0)[:, 0:T], ones_bf, act[:, 42 + r, 0:T],
                                                             start=(c == 0), stop=(c == KC - 1)),
                          reads=[A[42 + r], CONST], writes=[BK[0]])
                S.add("act", lambda e: e.activation(out=Ft[2][:, 0:T], in_=bank(0)[:, 0:T], func=AF.Ln,
                                                    scale=1.0 / D, bias=cvec[:, 0:1]),
                      reads=[BK[0], CV], writes=[FT[2]])
                S.add("act", lambda e: e.activation(out=Ft[2][:, 0:T], in_=Ft[2][:, 0:T], func=AF.Exp, scale=-0.5),
                      reads=[FT[2]], writes=[FT[2]])
                for c in range(KC):
                    q = "dve"
                    S.add(q, lambda e, c=c: e.scalar_tensor_tensor(out=hbf[:, c, 0:T], in0=xb[:, c, 0:T],
                                                                   scalar=gains[:, gi, c:c + 1], in1=Ft[2][:, 0:T],
                                                                   op0=ALU.mult, op1=ALU.mult),
                          reads=[XK[c], FT[2], CONST], writes=[H[c]])
            return fn

        def ffn(which):
            stages.append((None, norm(0 if which == 0 else 2)))
            wg = wgu[which]
            wd = wdn[which]
            wg3 = wg.rearrange("(c p) n -> p c n", p=128)
            for jp in range(FC // 2):
                pieces = [((KC, 512, 0, 256), wg3[:, :, jp * 256:jp * 256 + 256]),
                          ((KC, 512, 256, 512), wg3[:, :, DFF + jp * 256:DFF + jp * 256 + 256])]

                def fn(slot, jp=jp):
                    w3 = wview3(slot, KC, 512)
                    if jp == 0 and T == 512:
                        for kc in range(KC):
                            def mmk(e, kc=kc):
                                ins = None
                                for i in range(2):
                                    for (bb, off) in ((2 * i, i * 128), (2 * i + 1, 256 + i * 128)):
                                        ins = e.matmul(bank(bb)[:, 0:T], w3[:, kc, off:off + 128], hbf[:, kc, 0:T],
                                                       start=(kc == 0), stop=(kc == KC - 1))
                                return ins
                            S.add("pe", mmk, reads=[WS[slot], H[kc]], writes=[BK[0], BK[1], BK[2], BK[3]])
                    for i in range(2):
                        j = 2 * jp + i
                        bg = (jp % 2) * 4 + 2 * i
                        bu = bg + 1
                        for (bb, off) in ((bg, i * 128), (bu, 256 + i * 128)):
                            if jp == 0 and T == 512:
                                continue
                            def mm(e, bb=bb, off=off):
                                ins = None
                                for kc in range(KC):
                                    ins = e.matmul(bank(bb)[:, 0:T], w3[:, kc, off:off + 128], hbf[:, kc, 0:T],
                                                   start=(kc == 0), stop=(kc == KC - 1))
                                return ins
                            S.add("pe", mm, reads=[WS[slot], Hall], writes=[BK[bb]])
                        fi = i
                        S.add("act", lambda e, bg=bg, fi=fi: e.activation(out=Ft[fi][:, 0:T], in_=bank(bg)[:, 0:T], func=AF.Silu),
                              reads=[BK[bg]], writes=[FT[fi]])
                        S.add("dve", lambda e, bu=bu, fi=fi, j=j: e.tensor_tensor(out=act[:, j, 0:T], in0=Ft[fi][:, 0:T],
                                                                                in1=bank(bu)[:, 0:T], op=ALU.mult),
                              reads=[FT[fi], BK[bu]], writes=[A[j]])
                stages.append((pieces, fn, ("gu", which, jp)))
            wd3 = wd.rearrange("(k p) n -> p k n", p=128)
            for mg in range(8):
                for kh in range(2):
                    pieces = [((22, 256, 0, 256), wd3[:, kh * 22:(kh + 1) * 22, mg * 256:(mg + 1) * 256])]

                    def fn(slot, mg=mg, kh=kh):
                        w3 = wview3(slot, 22, 256)
                        for mi in range(2):
                            bb = (mg % 2) * 2 + mi
                            m = mg * 2 + mi

                            def mm(e, bb=bb, mi=mi):
                                ins = None
                                for k in range(22):
                                    ins = e.matmul(bank(bb)[:, 0:T], w3[:, k, mi * 128:(mi + 1) * 128], act[:, kh * 22 + k, 0:T],
                                                   start=(kh == 0 and k == 0), stop=(kh == 1 and k == 21))
                                return ins
                            S.add("pe", mm, reads=[WS[slot]] + A[kh * 22:(kh + 1) * 22], writes=[BK[bb]])
                            if kh == 1:
                                S.add("dve", lambda e, bb=bb, m=m: e.scalar_tensor_tensor(
                                    out=xb[:, m, 0:T], in0=bank(bb)[:, 0:T], scalar=0.5, in1=xb[:, m, 0:T],
                                    op0=ALU.mult, op1=ALU.add), reads=[BK[bb], XK[m]], writes=[XK[m]])
                                if which == 1:
                                    S.add("pool", lambda e, m=m: e.dma_start(out=outT[m * 128:(m + 1) * 128, o0:o0 + T], in_=xb[:, m, 0:T]),
                                          reads=[XK[m]], writes=[("outdram", m)], dma=("ost", m))
                                    if has_next:
                                        load_x_chunk(m, tn0, 512, "pool")
                                        if m == KC - 1:
                                            load_pos(tn0, 512, "pool")
                    stages.append((pieces, fn, ("dn", which, mg, kh)))

        if "A" in part:
            ffn(0)
        if "B" not in part:
            return
        if ti == 1:
            stages.append((None, lambda _: load_pos(t0, T)))
        stages.append((None, norm(1)))

        def st_rope(_):
            a = HT[0].ap
            k_ = HT[1].ap
            AB = HT[0]
            KB = HT[1]
            S.add("dve", lambda e: e.tensor_copy(out=a[:, 0:T], in_=posi[:, 0:T]), reads=[POSI], writes=[AB])
            S.add("dve", lambda e: e.tensor_scalar(out=a[:, 0:T], in0=a[:, 0:T], scalar1=ropec[:, 0:1], scalar2=None,
                                                   op0=ALU.mult), reads=[AB, CONST], writes=[AB])
            S.add("dve", lambda e: e.tensor_scalar(out=k_[:, 0:T], in0=a[:, 0:T], scalar1=float(1.0 / (2 * np.pi)),
                                                   scalar2=12582912.0, op0=ALU.mult, op1=ALU.add), reads=[AB], writes=[KB])
            S.add("dve", lambda e: e.tensor_scalar(out=k_[:, 0:T], in0=k_[:, 0:T], scalar1=12582912.0, scalar2=None,
                                                   op0=ALU.subtract), reads=[KB], writes=[KB])
            S.add("dve", lambda e: e.scalar_tensor_tensor(out=a[:, 0:T], in0=k_[:, 0:T], scalar=-TWO_PI_HI, in1=a[:, 0:T],
                                                          op0=ALU.mult, op1=ALU.add), reads=[KB, AB], writes=[AB])
            S.add("dve", lambda e: e.scalar_tensor_tensor(out=a[:, 0:T], in0=k_[:, 0:T], scalar=-float(TWO_PI_LO), in1=a[:, 0:T],
                                                          op0=ALU.mult, op1=ALU.add), reads=[KB, AB], writes=[AB])
            S.add("dve", lambda e: e.tensor_scalar(out=a[:, 0:T], in0=a[:, 0:T], scalar1=-PI_CLAMP, scalar2=PI_CLAMP,
                                                   op0=ALU.max, op1=ALU.min), reads=[AB], writes=[AB])
            S.add("act", lambda e: e.activation(out=sinT[:, 0:T], in_=a[:, 0:T], func=AF.Sin), reads=[AB], writes=[SIN])
            S.add("dve", lambda e: e.scalar_tensor_tensor(out=k_[:, 0:T], in0=a[:, 0:T], scalar=-1.0, in1=a[:, 0:T], op0=ALU.mult, op1=ALU.max),
                  reads=[AB], writes=[KB])
            S.add("act", lambda e: e.activation(out=cosT[:, 0:T], in_=k_[:, 0:T], func=AF.Sin, scale=-1.0, bias=cvec[:, 1:2]),
                  reads=[KB, CV], writes=[COS])
        stages.append((None, st_rope))

        win3 = win.rearrange("(c p) n -> p c n", p=128)

        def proj(slot, w3, off, bb):
            def mm(e):
                ins = None
                for kc in range(KC):
                    ins = e.matmul(bank(bb)[:, 0:T], w3[:, kc, off:off + 128], hbf[:, kc, 0:T],
                                   start=(kc == 0), stop=(kc == KC - 1))
                return ins
            S.add("pe", mm, reads=[WS[slot], Hall], writes=[BK[bb]])

        def qk_chain(slot, w3, off, bb, gcol, dst_ap, dst_buf, par, preproj=False):
            if par == 0:
                zb, q2b = FT[0], FT[1]
                z, q2 = Ft[0], Ft[1]
            else:
                zb, q2b = HT[0], HT[1]
                z, q2 = HT[0].ap, HT[1].ap
            sb2 = 4 + (bb % 2)
            if not preproj:
                proj(slot, w3, off, bb)
                yield
            S.add("act", lambda e: e.activation(out=z[:, 0:T], in_=bank(bb)[:, 0:T], func=AF.Copy), reads=[BK[bb]], writes=[zb])
            yield
            S.add("act", lambda e: e.activation(out=q2[:, 0:T], in_=bank(bb)[:, 0:T], func=AF.Square), reads=[BK[bb]], writes=[q2b])
            yield
            S.add("pe", lambda e: e.matmul(bank(sb2)[:, 0:T], hblk32, q2[:, 0:T], start=True, stop=True),
                  reads=[q2b, CONST], writes=[BK[sb2]])
            yield
            S.add("act", lambda e: e.activation(out=q2[:, 0:T], in_=bank(sb2)[:, 0:T], func=AF.Ln, scale=1.0 / 64, bias=cvec[:, 0:1]),
                  reads=[BK[sb2], CV], writes=[q2b])
            yield
            S.add("act", lambda e: e.activation(out=q2[:, 0:T], in_=q2[:, 0:T], func=AF.Exp, scale=-0.5), reads=[q2b], writes=[q2b])
            yield
            S.add("dve", lambda e: e.scalar_tensor_tensor(out=z[:, 0:T], in0=z[:, 0:T], scalar=qkg[:, gcol:gcol + 1], in1=q2[:, 0:T],
                                                          op0=ALU.mult, op1=ALU.mult), reads=[zb, q2b, CONST], writes=[zb])
            yield
            S.add("pe", lambda e: e.matmul(bank(sb2 + 2)[:, 0:T], rot32, z[:, 0:T], start=True, stop=True),
                  reads=[zb, CONST], writes=[BK[sb2 + 2]])
            yield
            S.add("dve", lambda e: e.tensor_tensor(out=q2[:, 0:T], in0=bank(sb2 + 2)[:, 0:T], in1=sinT[:, 0:T], op=ALU.mult),
                  reads=[BK[sb2 + 2], SIN], writes=[q2b])
            yield
            S.add("dve", lambda e: e.tensor_tensor(out=z[:, 0:T], in0=z[:, 0:T], in1=cosT[:, 0:T], op=ALU.mult),
                  reads=[zb, COS], writes=[zb])
            yield
            S.add("dve", lambda e: e.tensor_tensor(out=dst_ap, in0=z[:, 0:T], in1=q2[:, 0:T], op=ALU.add),
                  reads=[zb, q2b], writes=[dst_buf])
            yield

        if not halo:
            wq5 = win[:, OQ:OQ + 1024].rearrange("(kc p) (g c d) -> p kc c g d", p=128, g=2, c=8)
            for qg in range(2):
                pieces = []
                for ci in range(4):
                    for g in range(2):
                        pieces.append(((KC, 512, ci * 128 + g * 64, ci * 128 + g * 64 + 64), wq5[:, :, qg * 4 + ci, g, :]))

                def fn(slot, qg=qg):
                    w3 = wview3(slot, KC, 512)
                    pre = (qg == 0)
                    if pre:
                        for kc in range(KC):
                            def mmk(e, kc=kc):
                                ins = None
                                for ci in range(2):
                                    ins = e.matmul(bank(ci)[:, 0:T], w3[:, kc, ci * 128:(ci + 1) * 128], hbf[:, kc, 0:T],
                                                   start=(kc == 0), stop=(kc == KC - 1))
                                return ins
                            S.add("pe", mmk, reads=[WS[slot], H[kc]], writes=[BK[0], BK[1]])
                    pipeline([qk_chain(slot, w3, ci * 128, ci % 2, 0, act[:, qg * 4 + ci, 0:T], A[qg * 4 + ci], ci % 2,
                                       preproj=(pre and ci < 2))
                              for ci in range(4)], depth=2, offset=3)
                stages.append((pieces, fn, ("q", qg)))

        pieces = [((KC, 256, 0, 256), win3[:, :, OK_:OK_ + 256])]

        def fn_kv(slot):
            w3 = wview3(slot, KC, 256)
            kdst = kT[:, 128:128 + T]
            for _ in qk_chain(slot, w3, 0, 0, 1, kdst, Buf(None, [("kT", 1 + b) for b in range(nblk)]), 0):
                pass
            for b in range(nblk):
                bb = 2 + (b % 2)

                def mm(e, b=b, bb=bb):
                    ins = None
                    for kc in range(KC):
                        ins = e.matmul(bank(bb)[:, 0:128], hbf[:, kc, b * 128:(b + 1) * 128], w3[:, kc, 128:256],
                                       start=(kc == 0), stop=(kc == KC - 1))
                    return ins
                S.add("pe", mm, reads=[WS[slot], Hall], writes=[BK[bb]])
                S.add("act", lambda e, b=b, bb=bb: e.activation(out=Vpad[:, 1 + b, 0, 0:64], in_=bank(bb)[:, 0:64], func=AF.Copy),
                      reads=[BK[bb]], writes=[VPB[1 + b]])
                S.add("act", lambda e, b=b, bb=bb: e.activation(out=Vpad[:, 1 + b, 1, 64:128], in_=bank(bb)[:, 64:128], func=AF.Copy),
                      reads=[BK[bb]], writes=[VPB[1 + b]])
        stages.append((pieces, fn_kv, ("kv",)))

        if not halo:
            def attn_unit(qb, u, uidx):
                first = (ti == 1 and qb == 0)
                mk = amask[:, 1 if first else 0, :]
                es = uidx % 2
                Ev = Eviews[es]
                EBx = EBS[es]
                denb = FT[uidx % 2]
                den = Ft[uidx % 2]
                for g in range(2):
                    d2 = (0, 1, 3)[rot_state["sc"] % 3]
                    rot_state["sc"] += 1
                    PSd = PS[d2]
                    bks = [BK[2 * d2], BK[2 * d2 + 1]]

                    def mm(e, g=g, PSd=PSd):
                        ins = None
                        for kb in range(2):
                            for ci in range(4):
                                cp = u * 4 + ci
                                ins = e.matmul(PSd[:, kb * 512 + ci * 128: kb * 512 + ci * 128 + 128],
                                               kT[g * 64:(g + 1) * 64, (qb + kb) * 128:(qb + kb + 1) * 128],
                                               act[g * 64:(g + 1) * 64, cp, qb * 128:(qb + 1) * 128],
                                               start=True, stop=True)
                        return ins
                    S.add("pe", mm, reads=[KTB[qb], KTB[qb + 1]] + A[u * 4:u * 4 + 4], writes=bks)
                    yield
                    S.add("act", lambda e, g=g, PSd=PSd: e.activation(out=Ev[:, g, :], in_=PSd[:, :], func=AF.Exp, scale=0.125),
                          reads=bks, writes=[EBx[g]])
                    yield
                    S.add("dve", lambda e, g=g: e.tensor_tensor(out=Ev[:, g, :], in0=Ev[:, g, :], in1=mk, op=ALU.mult),
                          reads=[EBx[g], CONST], writes=[EBx[g]])
                    yield

                def pv(e):
                    ins = None
                    n = 0
                    for g in range(2):
                        for kb in range(2):
                            ins = e.matmul(bank(4)[:, :], Vpad[:, qb + kb, g, :], Ev[:, g, kb * 512:(kb + 1) * 512],
                                           start=(n == 0), stop=(n == 3))
                            n += 1
                    n = 0
                    for g in range(2):
                        for kb in range(2):
                            ins = e.matmul(bank(5)[:, :], onespad[:, g, :], Ev[:, g, kb * 512:(kb + 1) * 512],
                                           start=(n == 0), stop=(n == 3))
                            n += 1
                    return ins
                S.add("pe", pv, reads=[EBx[0], EBx[1], VPB[qb], VPB[qb + 1], "onespad"], writes=[BK[4], BK[5]])
                yield
                den3 = den[:, :].rearrange("p (c q) -> p c q", c=4)
                S.add("dve", lambda e: e.tensor_tensor(
                    out=den3, in0=bank(5)[:, :].rearrange("p (c q) -> p c q", c=4),
                    in1=esink[:, u * 4:u * 4 + 4].unsqueeze(2).to_broadcast([128, 4, 128]), op=ALU.add),
                    reads=[BK[5], "esink"], writes=[denb])
                yield
                S.add("act", lambda e: e.activation(out=den[:, :], in_=den[:, :], func=AF.Ln), reads=[denb], writes=[denb])
                yield
                S.add("act", lambda e: e.activation(out=den[:, :], in_=den[:, :], func=AF.Exp, scale=-1.0), reads=[denb], writes=[denb])
                yield
                S.add("dve", lambda e: e.tensor_tensor(
                    out=act[:, 8 + u * 4:8 + u * 4 + 4, qb * 128:(qb + 1) * 128],
                    in0=bank(4)[:, :].rearrange("p (c q) -> p c q", c=4), in1=den3, op=ALU.mult),
                    reads=[BK[4], denb], writes=A[8 + u * 4:8 + u * 4 + 4])
                yield

            def st_attn(_):
                units = []
                n = 0
                for qb in range(nblk):
                    for u in range(2):
                        units.append(attn_unit(qb, u, n))
                        n += 1
                pipeline(units, depth=2, offset=4)
            stages.append((None, st_attn))

        def st_shift(_):
            S.add("dve", lambda e: e.tensor_copy(out=kT[:, 0:128], in_=kT[:, nblk * 128:(nblk + 1) * 128]),
                  reads=[KTB[nblk]], writes=[KTB[0]])
            S.add("dve", lambda e: e.tensor_copy(out=Vpad[:, 0, :, :], in_=Vpad[:, nblk, :, :]),
                  reads=[VPB[nblk]], writes=[VPB[0]])
        stages.append((None, st_shift))

        for half in range(2):
            pieces = [((KC, 512, 0, 512), win3[:, :, OGI + half * 512:OGI + (half + 1) * 512])]

            def fn_gi(slot, half=half):
                w3 = wview3(slot, KC, 512)
                for b in range(nblk):
                    bb = (b % 2)

                    def mm(e, b=b, bb=bb):
                        ins = None
                        for kc in range(KC):
                            ins = e.matmul(bank(bb)[:, :], hbf[:, kc, b * 128:(b + 1) * 128], w3[:, kc, :],
                                           start=(kc == 0), stop=(kc == KC - 1))
                        return ins
                    S.add("pe", mm, reads=[WS[slot], Hall], writes=[BK[bb]])
                    S.add("act", lambda e, b=b, bb=bb: e.activation(out=vhv[:, b, half * 512:(half + 1) * 512], in_=bank(bb)[:, :], func=AF.Copy),
                          reads=[BK[bb]], writes=[VH[b]])
            stages.append((pieces, fn_gi, ("gi", half)))

        for hd in range(8):
            pieces = [((KC, 384, 0, 128), win3[:, :, OGF + hd * 128:OGF + (hd + 1) * 128]),
                      ((KC, 384, 128, 256), win3[:, :, OGQ + hd * 128:OGQ + (hd + 1) * 128]),
                      ((KC, 384, 256, 384), win3[:, :, OGO + hd * 128:OGO + (hd + 1) * 128])]

            def fn_hg(slot, hd=hd):
                par = hd % 2
                R_ = HGSET[par]
                KtTx, KTTx, Qhx, QHx, Ktx, KTx, ATmx, ATBx = (R_["KtT"], R_["KTT"], R_["Qh"], R_["QH"], R_["Kt"],
                                                              R_["KT"], R_["ATm"], R_["ATB"])
                w3 = wview3(slot, KC, 384)
                tf, tl, tb, teb, tq = HTSET[par]
                tenb = tl
                ob = 3 - par
                ab = 4 + par
                ub = 6 + par
                pstr_ = pstrs[par]
                Khx = Kh[par]
                KHx = Buf(Khx[:], [("khat", par)])

                def pbank():
                    b = rot_state["pj"] % 2
                    rot_state["pj"] += 1
                    return b
                pb = pbank()
                proj(slot, w3, 0, pb)
                yield
                if not halo:
                    pb2 = pbank()
                    proj(slot, w3, 128, pb2)
                    yield
                S.add("act", lambda e: e.activation(out=tf.ap[:, 0:T], in_=bank(pb)[:, 0:T], func=AF.Sigmoid), reads=[BK[pb]], writes=[tf])
                yield
                if not halo:
                    S.add("act", lambda e: e.activation(out=tq.ap[:, 0:T], in_=bank(pb2)[:, 0:T], func=AF.Sigmoid), reads=[BK[pb2]], writes=[tq])
                    yield
                    S.add("dve", lambda e: e.tensor_tensor(out=tq.ap[:, 0:T], in0=tq.ap[:, 0:T], in1=bank(pb2)[:, 0:T], op=ALU.mult),
                          reads=[tq, BK[pb2]], writes=[tq])
                    yield
                    pb3 = pbank()
                    proj(slot, w3, 256, pb3)
                    yield
                    S.add("act", lambda e: e.activation(out=tb.ap[:, 0:T], in_=bank(pb3)[:, 0:T], func=AF.Sigmoid), reads=[BK[pb3]], writes=[tb])
                    yield
                    S.add("dve", lambda e: e.tensor_tensor(out=act[:, 24 + hd, 0:T], in0=tb.ap[:, 0:T], in1=bank(pb3)[:, 0:T], op=ALU.mult),
                          reads=[tb, BK[pb3]], writes=[A[24 + hd]])
                    yield
                S.add("dve", lambda e: e.tensor_scalar(out=tf.ap[:, 0:T], in0=tf.ap[:, 0:T], scalar1=lbv[:, 1, hd:hd + 1],
                                                       scalar2=lbv[:, 0, hd:hd + 1], op0=ALU.mult, op1=ALU.add),
                      reads=[tf, LBV], writes=[tf])
                yield
                S.add("act", lambda e: e.activation(out=tl.ap[:, 0:T], in_=tf.ap[:, 0:T], func=AF.Ln), reads=[tf], writes=[tl])
                yield
                S.add("dve", lambda e: e.tensor_tensor_scan(out=tb.ap[:, 0:T], data0=rmask[:, 0:T], data1=tl.ap[:, 0:T], initial=0.0,
                                                            op0=ALU.mult, op1=ALU.add), reads=[tl, CONST], writes=[tb])
                yield
                S.add("act", lambda e: e.activation(out=teb.ap[:, 0:T], in_=tb.ap[:, 0:T], func=AF.Exp), reads=[tb], writes=[teb])
                yield
                S.add("act", lambda e: e.activation(out=tenb.ap[:, 0:T], in_=tb.ap[:, 0:T], func=AF.Exp, scale=-1.0), reads=[tb], writes=[tenb])
                yield
                S.add("dve", lambda e: e.tensor_scalar(out=tf.ap[:, 0:T], in0=tf.ap[:, 0:T], scalar1=-1.0, scalar2=1.0,
                                                       op0=ALU.mult, op1=ALU.add), reads=[tf], writes=[tf])
                yield
                S.add("dve", lambda e: e.tensor_tensor(out=Ktx[:, 0:T], in0=tf.ap[:, 0:T], in1=tenb.ap[:, 0:T], op=ALU.mult),
                      reads=[tf, tenb], writes=[KTx])
                yield
                if not halo:
                    S.add("dve", lambda e: e.tensor_tensor(out=Qhx[:, 0:T], in0=tq.ap[:, 0:T], in1=teb.ap[:, 0:T], op=ALU.mult),
                          reads=[tq, teb], writes=[QHx])
                    yield
                nch = T // 64
                S.add("dve", lambda e: e.tensor_tensor(
                    out=Khx[:, 0:T].rearrange("p (c t) -> p c t", t=64), in0=Ktx[:, 0:T].rearrange("p (c t) -> p c t", t=64),
                    in1=teb.ap[:, 0:T].rearrange("p (c t) -> p c t", t=64)[:, :, 63:64].to_broadcast([128, nch, 64]), op=ALU.mult),
                    reads=[KTx, teb], writes=[KHx])
                yield
                for b in range(nblk):
                    S.add("pe", lambda e, b=b: e.transpose(pstr_[:, b * 128:(b + 1) * 128], Khx[:, b * 128:(b + 1) * 128], ident_bf),
                          reads=[KHx, CONST], writes=[BK[ab]])
                yield
                S.add("act", lambda e: e.activation(out=KtTx[:, 0:nblk, :], in_=pstr_[:, 0:nblk * 128].rearrange("p (b d) -> p b d", b=nblk),
                                                    func=AF.Copy), reads=[BK[ab]], writes=KTTx[0:nblk])
                yield
                nchk = 2 * nblk

                def u_mm(c):
                    b_, c2_ = divmod(c, 2)
                    ubc = ub if c % 2 == 0 else ab
                    S.add("pe", lambda e: e.matmul(bank(ubc)[:, 0:128], KtTx[c2_ * 64:(c2_ + 1) * 64, b_, :],
                                                   vhv[c2_ * 64:(c2_ + 1) * 64, b_, hd * 128:(hd + 1) * 128],
                                                   start=True, stop=True), reads=[KTTx[b_], VH[b_]], writes=[BK[ubc]])

                def s_in(c):
                    if c == 0:
                        return Sbf[:, hd, :], SBF[hd]
                    return Sring[:, par, (c - 1) % 2, :], SRB[par][(c - 1) % 2]

                for c in range(min(1, nchk)):
                    u_mm(c)
                yield
                for c in range(nchk):
                    b = c // 2
                    col = c * 64
                    ubc = ub if c % 2 == 0 else ab
                    if c % 2 == 1 and c + 1 < nchk:
                        u_mm(c + 1)
                        yield
                    if c % 2 == 0 and not halo:
                        S.add("pe", lambda e, b=b: e.matmul(bank(ab)[:, 0:128], Ktx[:, b * 128:(b + 1) * 128], Qhx[:, b * 128:(b + 1) * 128],
                                                            start=True, stop=True), reads=[KTx, QHx], writes=[BK[ab]])
                        yield
                        S.add("dve", lambda e, b=b: e.tensor_tensor(out=ATmx[:, b, :], in0=bank(ab)[:, 0:128], in1=hgmask_bf, op=ALU.mult),
                              reads=[BK[ab], CONST], writes=[ATBx[b]])
                        yield
                    if c % 2 == 0 and c + 1 < nchk:
                        u_mm(c + 1)
                        yield
                        S.add("pe", lambda e, b=b: e.matmul(bank(ob)[:, b * 128:(b + 1) * 128], vhv[:, b, hd * 128:(hd + 1) * 128], ATmx[:, b, :],
                                                            start=True, stop=False), reads=[VH[b], ATBx[b]], writes=[BK[ob]])
                        yield
                    if not halo:
                        sap, sbuf_ = s_in(c)
                        S.add("pe", lambda e, col=col, c=c, sap=sap: e.matmul(bank(ob)[:, col:col + 64], sap, Qhx[:, col:col + 64],
                                                                             start=False, stop=(c % 2 == 1)),
                              reads=[sbuf_, QHx], writes=[BK[ob]])
                        yield
                    if c == nchk - 1:
                        dap, dbuf = Sbf[:, hd, :], SBF[hd]
                    else:
                        dap, dbuf = Sring[:, par, c % 2, :], SRB[par][c % 2]
                    S.add("dve", lambda e, col=col, ubc=ubc, dap=dap: e.scalar_tensor_tensor(
                        out=dap, in0=S32[:, hd, :], scalar=teb.ap[:, col + 63:col + 64], in1=bank(ubc)[:, 0:128],
                        op0=ALU.mult, op1=ALU.add), reads=[BK[ubc], SB32[hd], teb], writes=[dbuf])
                    yield
                    S.add("dve", lambda e, col=col, ubc=ubc: e.scalar_tensor_tensor(
                        out=S32[:, hd, :], in0=S32[:, hd, :], scalar=teb.ap[:, col + 63:col + 64], in1=bank(ubc)[:, 0:128],
                        op0=ALU.mult, op1=ALU.add), reads=[BK[ubc], SB32[hd], teb], writes=[SB32[hd]])
                    yield
                if not halo:
                    o32, osq, lnv = tf, tl, tb
                    S.add("act", lambda e: e.activation(out=o32.ap[:, 0:T], in_=bank(ob)[:, 0:T], func=AF.Copy), reads=[BK[ob]], writes=[o32])
                    yield
                    S.add("act", lambda e: e.activation(out=osq.ap[:, 0:T], in_=bank(ob)[:, 0:T], func=AF.Square), reads=[BK[ob]], writes=[osq])
                    yield
                    pb4 = pbank()
                    S.add("pe", lambda e: e.matmul(bank(pb4)[:, 0:T], ones32, osq.ap[:, 0:T], start=True, stop=True),
                          reads=[osq, CONST], writes=[BK[pb4]])
                    yield
                    S.add("act", lambda e: e.activation(out=lnv.ap[:, 0:T], in_=bank(pb4)[:, 0:T], func=AF.Ln, scale=1.0 / 128, bias=cvec[:, 0:1]),
                          reads=[BK[pb4], CV], writes=[lnv])
                    yield
                    S.add("act", lambda e: e.activation(out=lnv.ap[:, 0:T], in_=lnv.ap[:, 0:T], func=AF.Exp, scale=-0.5), reads=[lnv], writes=[lnv])
                    yield
                    S.add("dve", lambda e: e.scalar_tensor_tensor(out=o32.ap[:, 0:T], in0=o32.ap[:, 0:T], scalar=hgg[:, 0:1], in1=lnv.ap[:, 0:T],
                                                                  op0=ALU.mult, op1=ALU.mult), reads=[o32, lnv, CONST], writes=[o32])
                    yield
                    S.add("dve", lambda e: e.tensor_tensor(out=act[:, 24 + hd, 0:T], in0=o32.ap[:, 0:T], in1=act[:, 24 + hd, 0:T], op=ALU.mult),
                          reads=[o32, A[24 + hd]], writes=[A[24 + hd]])
                    yield
            stages.append((pieces, fn_hg, ("hg", hd), "pipe_hg"))

        if halo:
            return

        wA4 = wA.rearrange("(g c d) n -> d g c n", g=2, c=8)
        wR3 = wR.rearrange("(c p) n -> p c n", p=128)

        def merged_slot(m):
            return m if m < 8 else 16 + (m - 8)

        for mp in range(8):
            def ar_dma(base, mp=mp):
                w3 = view3(base, 16, 256)
                res = []
                for g in range(2):
                    res.append((w3[g * 64:(g + 1) * 64, 0:8, :], wA4[:, g, :, mp * 256:(mp + 1) * 256]))
                res.append((w3[:, 8:16, :], wR3[:, :, mp * 256:(mp + 1) * 256]))
                return res
            arslot = {}

            def fn_ar(slot, arslot=arslot):
                arslot["s"] = slot
            stages.append((("raw", ar_dma), fn_ar, ("ar", mp)))
            piecesBR = [((KC, 512, 0, 256), win3[:, :, OBR + mp * 256:OBR + mp * 256 + 256]),
                        ((KC, 512, 256, 512), win3[:, :, OBR + D + mp * 256:OBR + D + mp * 256 + 256])]

            def fn_br(slot, mp=mp, arslot=arslot):
                sA = arslot["s"]
                wa3 = wview3(sA, 16, 256)
                w3 = wview3(slot, KC, 512)
                for mi in range(2):
                    m = mp * 2 + mi
                    off = mi * 128
                    b0 = (mi % 2) * 4
                    pAb, pRb, gAb, gRb = b0, b0 + 1, b0 + 2, b0 + 3

                    def mmA(e, off=off, pAb=pAb):
                        ins = None
                        for c in range(8):
                            ins = e.matmul(bank(pAb)[:, 0:T], wa3[:, c, off:off + 128], act[:, 8 + c, 0:T], start=(c == 0), stop=(c == 7))
                        return ins
                    S.add("pe", mmA, reads=[WS[sA]] + A[8:16], writes=[BK[pAb]])

                    def mmR(e, off=off, pRb=pRb):
                        ins = None
                        for c in range(8):
                            ins = e.matmul(bank(pRb)[:, 0:T], wa3[:, 8 + c, off:off + 128], act[:, 24 + c, 0:T], start=(c == 0), stop=(c == 7))
                        return ins
                    S.add("pe", mmR, reads=[WS[sA]] + A[24:32], writes=[BK[pRb]])
                    proj(slot, w3, mi * 128, gAb)
                    proj(slot, w3, 256 + mi * 128, gRb)
                    S.add("act", lambda e, gAb=gAb: e.activation(out=Ft[0][:, 0:T], in_=bank(gAb)[:, 0:T], func=AF.Sigmoid), reads=[BK[gAb]], writes=[FT[0]])
                    S.add("act", lambda e, gRb=gRb: e.activation(out=Ft[1][:, 0:T], in_=bank(gRb)[:, 0:T], func=AF.Sigmoid), reads=[BK[gRb]], writes=[FT[1]])
                    S.add("dve", lambda e, pAb=pAb: e.tensor_tensor(out=Ft[0][:, 0:T], in0=Ft[0][:, 0:T], in1=bank(pAb)[:, 0:T], op=ALU.mult),
                          reads=[FT[0], BK[pAb]], writes=[FT[0]])
                    S.add("dve", lambda e, pRb=pRb: e.tensor_tensor(out=Ft[1][:, 0:T], in0=Ft[1][:, 0:T], in1=bank(pRb)[:, 0:T], op=ALU.mult),
                          reads=[FT[1], BK[pRb]], writes=[FT[1]])
                    ms = merged_slot(m)
                    S.add("dve", lambda e, ms=ms: e.tensor_tensor(out=act[:, ms, 0:T], in0=Ft[0][:, 0:T], in1=Ft[1][:, 0:T], op=ALU.add),
                          reads=[FT[0], FT[1]], writes=[A[ms]])
            stages.append((piecesBR, fn_br, ("br", mp)))

        wO3 = wO.rearrange("(c p) n -> p c n", p=128)
        MERG = [A[merged_slot(m)] for m in range(16)]
        for mq in range(4):
            pieces = [((KC, 512, 0, 512), wO3[:, :, mq * 512:(mq + 1) * 512])]

            def fn_o(slot, mq=mq):
                w3 = wview3(slot, KC, 512)
                for mi in range(4):
                    m = mq * 4 + mi
                    bb = mi % 4

                    def mm(e, mi=mi, bb=bb):
                        ins = None
                        for kc in range(KC):
                            ins = e.matmul(bank(bb)[:, 0:T], w3[:, kc, mi * 128:(mi + 1) * 128], act[:, merged_slot(kc), 0:T],
                                           start=(kc == 0), stop=(kc == KC - 1))
                        return ins
                    S.add("pe", mm, reads=[WS[slot]] + MERG, writes=[BK[bb]])
                    S.add("dve", lambda e, m=m, bb=bb: e.tensor_tensor(out=xb[:, m, 0:T], in0=xb[:, m, 0:T], in1=bank(bb)[:, 0:T], op=ALU.add),
                          reads=[XK[m], BK[bb]], writes=[XK[m]])
            stages.append((pieces, fn_o, ("wo", mq)))

        ffn(1)

    tiles = [(0, 0, HALO, True)] + [(1 + i, HALO + 512 * i, 512, False) for i in range(8)]
    if n_tiles >= 2:
        tile_prog(*tiles[1], "A")
        tile_prog(*tiles[0], "AB")
        tile_prog(*tiles[1], "B")
        for tl_ in tiles[2:n_tiles]:
            tile_prog(*tl_, "AB")
    else:
        tile_prog(*tiles[0], "AB")

    def pieces_to_list(pieces, base):
        if isinstance(pieces, tuple) and pieces[0] == "raw":
            return pieces[1](base)
        return [(view3(base, a, b)[:, :, c0:c1], src) for (a, b, c0, c1), src in pieces]

    scratch = {}
    ncv = 0
    for st in stages:
        if st[0] is None:
            continue
        sid = st[2]
        if sid in scratch:
            continue
        scr = nc.dram_tensor("scr_" + "_".join(str(v) for v in sid), [128, SLOT_ELEMS], BF16).ap()
        scratch[sid] = scr
        tok = ncv % 8
        for dst, src in pieces_to_list(st[0], scr):
            S.add("pool", lambda e, dst=dst, src=src: e.dma_start(out=dst, in_=src),
                  writes=[("scr", sid), ("cvtok", tok)], dma=("cv", tok))
        ncv += 1

    wstages = [i for i, st in enumerate(stages) if st[0] is not None]
    slot_of = {si: n % NSLOT for n, si in enumerate(wstages)}
    planned = [0]

    def stage_elems(pieces):
        if isinstance(pieces, tuple) and pieces[0] == "raw":
            return 16 * 256
        return max(a * b for (a, b, c0, c1), src in pieces)

    def plan_dma(upto_n):
        while planned[0] < len(wstages) and planned[0] <= upto_n:
            si = wstages[planned[0]]
            slot = slot_of[si]
            sid = stages[si][2]
            ne = stage_elems(stages[si][0])
            S.add("sp", lambda e, slot=slot, sid=sid, ne=ne: e.dma_start(out=wsl[slot][:, 0:ne], in_=scratch[sid][:, 0:ne]),
                  reads=[("scr", sid)], writes=[WS[slot]], dma=("wsem", slot))
            planned[0] += 1

    widx = {si: n for n, si in enumerate(wstages)}
    si = 0
    while si < len(stages):
        st = stages[si]
        if len(st) > 3:
            grp = [si]
            while grp[-1] + 1 < len(stages) and len(stages[grp[-1] + 1]) > 3 and stages[grp[-1] + 1][3] == st[3]:
                grp.append(grp[-1] + 1)

            def mk(sj):
                def g():
                    plan_dma(widx[sj] + NSLOT - 2)
                    yield from stages[sj][1](slot_of[sj])
                return g()
            pipeline([mk(sj) for sj in grp], depth=2, offset=24)
            si = grp[-1] + 1
            continue
        if st[0] is not None:
            plan_dma(widx[si] + NSLOT - 2)
            st[1](slot_of[si])
        else:
            st[1](None)
        si += 1
    S.add("sp", lambda e: e.nop(), reads=[("outdram", m) for m in range(KC)])
    S.emit()
    return nc


_CACHE = {}


def _consts():
    c = np.zeros((128, 5, 128), np.float32)
    c[:, 0, :] = np.eye(128, dtype=np.float32)
    c[:, 1, :] = 1.0
    for blk in range(2):
        for i in range(32):
            c[blk * 64 + i + 32, 2, blk * 64 + i] = -1.0
            c[blk * 64 + i, 2, blk * 64 + i + 32] = 1.0
    s = np.arange(128)[:, None]
    t = np.arange(128)[None, :]
    c[:, 3, :] = ((s // 64 == t // 64) & (s <= t)).astype(np.float32)
    c[:, 4, :] = (s // 64 == t // 64).astype(np.float32)
    rmask = np.ones((128, 512), np.float32)
    rmask[:, ::64] = 0.0
    inv_freq = (np.float32(10000.0) ** (-np.arange(32, dtype=np.float32) * np.float32(2.0) / np.float32(64))).astype(np.float32)
    ropec = np.tile(inv_freq, 4).reshape(128, 1).astype(np.float32)
    j = np.arange(128)[:, None]
    i = np.arange(128)[None, :]
    prev = (j > i).astype(np.float32)
    cur = (j <= i).astype(np.float32)
    am = np.zeros((128, 2, 2, 4, 128), np.float32)
    am[:, 0, 0] = prev[:, None, :]
    am[:, 0, 1] = cur[:, None, :]
    am[:, 1, 0] = prev[:, None, :]
    am[:, 1, 1] = cur[:, None, :]
    return c, rmask, ropec, am.reshape(128, 2, 1024)


def kernel(x, positions, lb_table, ffn1_norm, ffn1_w_gu, ffn1_w_down, mix_norm, w_in, q_norm, k_norm,
           sinks, hg_out_norm, w_attn_branch, w_hg_branch, w_out, ffn2_norm, ffn2_w_gu, ffn2_w_down,
           _n_tiles=9, _cores=None):
    f = lambda a: np.ascontiguousarray(np.asarray(a), dtype=np.float32)
    x = f(x)
    positions = np.ascontiguousarray(np.asarray(positions), dtype=np.int32)
    key = _n_tiles
    if key not in _CACHE:
        _CACHE[key] = build(_n_tiles)
    nc = _CACHE[key]
    cm, rmask, ropec, am = _consts()
    gains = np.stack([f(ffn1_norm)[0].reshape(KC, 128).T, f(mix_norm)[0].reshape(KC, 128).T,
                      f(ffn2_norm)[0].reshape(KC, 128).T], axis=1)
    qkg = np.stack([np.tile(f(q_norm)[0], 2), np.tile(f(k_norm)[0], 2)], axis=1)
    hgg = f(hg_out_norm)[0].reshape(128, 1)
    lbt = f(lb_table).reshape(2, 8, 128).transpose(2, 0, 1)
    shared = {
        "wgu1": f(ffn1_w_gu)[0], "wgu2": f(ffn2_w_gu)[0], "wd1": f(ffn1_w_down)[0], "wd2": f(ffn2_w_down)[0],
        "win": f(w_in)[0], "wA": f(w_attn_branch)[0], "wR": f(w_hg_branch)[0], "wO": f(w_out)[0],
        "gains": np.ascontiguousarray(gains), "qkg": np.ascontiguousarray(qkg), "hgg": np.ascontiguousarray(hgg),
        "lbt": np.ascontiguousarray(lbt), "sinks": f(sinks).reshape(1, 16),
        "cmat": cm, "rmask": rmask, "ropec": ropec,
    }
    cores = list(range(8)) if _cores is None else _cores
    in_maps = []
    for cid in cores:
        b = cid // 4
        s0 = (cid % 4) * SEG
        xt = np.zeros((D, NT), np.float32)
        ps = np.zeros((1, NT), np.int32)
        if s0 > 0:
            xt[:, :] = x[b, s0 - HALO:s0 + SEG, :].T
            ps[0, :] = positions[b, s0 - HALO:s0 + SEG]
        else:
            xt[:, HALO:] = x[b, 0:SEG, :].T
            ps[0, HALO:] = positions[b, 0:SEG]
        am_c = am.copy()
        if s0 == 0:
            am_c[:, 1, 0:512] = 0.0
        m = dict(shared)
        m["xT"] = xt
        m["pos"] = ps
        m["amask"] = am_c
        in_maps.append(m)
    res = run_bass_kernel_spmd(nc, in_maps, core_ids=list(range(len(cores))))
    out = np.zeros((2, SEQ, D), np.float32)
    for k, cid in enumerate(cores):
        b = cid // 4
        s0 = (cid % 4) * SEG
        out[b, s0:s0 + SEG, :] = res.results[k]["outT"].T
    return out
```

```python
import numpy as np
import concourse.bass as bass
import concourse.mybir as mybir
from concourse.bass_utils import run_bass_kernel_spmd

F32 = mybir.dt.float32
BF16 = mybir.dt.bfloat16
I32 = mybir.dt.int32
AF = mybir.ActivationFunctionType
ALU = mybir.AluOpType

D = 2048
DFF = 5632
KC = D // 128
FC = DFF // 128
SEQ = 16384
SEG = 4096
HALO = 128
NT = SEG + HALO
EPS = 1e-6
TMAX = 512
NSLOT = 4
SLOT_ELEMS = 8192
OQ, OK_, OV, OGQ, OGF, OGI, OGO, OBR = 0, 1024, 1152, 1280, 2304, 3328, 4352, 5376
TWO_PI_HI = 6.28125
TWO_PI_LO = 2.0 * np.pi - 6.28125
PI_CLAMP = 3.1415925


def pipeline(gens, depth=2, offset=0):
    it = iter(gens)
    active = []
    while True:
        while len(active) < depth:
            g = next(it, None)
            if g is None:
                break
            if active:
                for _ in range(offset):
                    for a in list(active):
                        try:
                            next(a)
                        except StopIteration:
                            active.remove(a)
            active.append(g)
        if not active:
            break
        for a in list(active):
            try:
                next(a)
            except StopIteration:
                active.remove(a)


class Buf:
    __slots__ = ("ap", "keys")

    def __init__(self, ap, keys):
        self.ap = ap
        self.keys = tuple(keys)


def _keys(items):
    out = []
    for it in items:
        if isinstance(it, Buf):
            out.extend(it.keys)
        else:
            out.append(it)
    return out


class Sched:
    ENGS = ("pe", "act", "dve", "pool", "sp")

    def __init__(self, nc):
        self.nc = nc
        self.ops = {e: [] for e in self.ENGS}
        self.lastw = {}
        self.readers = {}
        self.dma_cnt = {}

    def add(self, eng, fn, reads=(), writes=(), dma=None):
        rk = _keys(reads)
        wk = _keys(writes)
        deps = {}

        def need(ev):
            k = (ev[0], ev[1])
            if deps.get(k, -1) < ev[2]:
                deps[k] = ev[2]

        for k in rk:
            ev = self.lastw.get(k)
            if ev is not None:
                need(ev)
        for k in wk:
            ev = self.lastw.get(k)
            if ev is not None:
                need(ev)
            rd = self.readers.get(k)
            if rd:
                for kk, vv in rd.items():
                    need((kk[0], kk[1], vv))
        if eng == "pe":
            deps.pop(("c", "pe"), None)
        idx = len(self.ops[eng])
        if dma is not None:
            c = self.dma_cnt.get(dma, 0) + 1
            self.dma_cnt[dma] = c
            ev = ("d", dma, 16 * c)
        else:
            ev = ("c", eng, idx)
        for (ty, who), v in deps.items():
            if ty == "c":
                self.ops[who][v][3] = True
        self.ops[eng].append([fn, deps, dma, False])
        for k in wk:
            self.lastw[k] = ev
            self.readers[k] = {}
        wset = set(wk)
        for k in rk:
            if k in wset:
                continue
            rd = self.readers.setdefault(k, {})
            kk = (ev[0], ev[1])
            if rd.get(kk, -1) < ev[2]:
                rd[kk] = ev[2]
        return ev

    def emit(self):
        nc = self.nc
        sems = {e: nc.alloc_semaphore("s_" + e) for e in ("pe", "act", "dve", "pool")}
        dsem = {k: nc.alloc_semaphore("d_%d" % i) for i, k in enumerate(self.dma_cnt)}
        cum = {}
        for e, ops in self.ops.items():
            c = 0
            arr = []
            for o in ops:
                if o[3] and o[2] is None:
                    c += 1
                arr.append(c)
            cum[e] = arr
        ops_all = self.ops

        def run(e, engine):
            seen = {}
            for fn, deps, dma, sig in ops_all[e]:
                for (ty, who), v in deps.items():
                    if ty == "c":
                        s = sems[who]
                        val = cum[who][v]
                    else:
                        s = dsem[who]
                        val = v
                    if seen.get((ty, who), 0) >= val:
                        continue
                    seen[(ty, who)] = val
                    engine.wait_ge(s, val)
                ins = fn(engine)
                if dma is not None:
                    ins.then_inc(dsem[dma], 16)
                elif sig:
                    ins.then_inc(sems[e], 1)

        with nc.Block() as block:
            @block.tensor
            def _(eng):
                run("pe", eng)

            @block.scalar
            def _(eng):
                run("act", eng)

            @block.vector
            def _(eng):
                run("dve", eng)

            @block.gpsimd
            def _(eng):
                run("pool", eng)

            @block.sync
            def _(eng):
                run("sp", eng)


def build(n_tiles=9):
    nc = bass.Bass("TRN2", target_bir_lowering=False)
    S = Sched(nc)

    def din(name, shape, dt=F32):
        return nc.dram_tensor(name, list(shape), dt, kind="ExternalInput").ap()

    xT = din("xT", [D, NT])
    pos_d = din("pos", [1, NT], I32)
    wgu = [din("wgu1", [D, 2 * DFF]), din("wgu2", [D, 2 * DFF])]
    wdn = [din("wd1", [DFF, D]), din("wd2", [DFF, D])]
    win = din("win", [D, 9472])
    wA = din("wA", [1024, D])
    wR = din("wR", [1024, D])
    wO = din("wO", [D, D])
    gains_d = din("gains", [128, 3, KC])
    qkg_d = din("qkg", [128, 2])
    hgg_d = din("hgg", [128, 1])
    lbt_d = din("lbt", [128, 2, 8])
    sinks_d = din("sinks", [1, 16])
    cmat_d = din("cmat", [128, 5, 128])
    amask_d = din("amask", [128, 2, 1024])
    rmask_d = din("rmask", [128, 512])
    ropec_d = din("ropec", [128, 1])
    outT = nc.dram_tensor("outT", [D, SEG], F32, kind="ExternalOutput").ap()

    def sb(name, shape, dt):
        return nc.alloc_sbuf_tensor("sb_" + name, shape, dt)
    x32 = sb("x32", [128, KC, TMAX], F32)
    hbf = sb("hbf", [128, KC, TMAX], BF16)
    act = sb("act", [128, FC, TMAX], BF16)
    wsl = [sb("wsl%d" % i, [128, SLOT_ELEMS], BF16) for i in range(NSLOT)]
    Ft = [sb("ftmp%d" % i, [128, TMAX], F32) for i in range(3)]
    x32h = sb("x32h", [128, KC, HALO], F32)
    cm32 = sb("cm32", [128, 5, 128], F32)
    cmbf = sb("cmbf", [128, 5, 128], BF16)
    amask = sb("amask", [128, 2, 1024], BF16)
    rmask = sb("rmask", [128, 512], F32)
    cosT = sb("cosT", [128, TMAX], F32)
    sinT = sb("sinT", [128, TMAX], F32)
    posi = sb("posi", [128, TMAX], I32)
    gains = sb("gains", [128, 3, KC], F32)
    qkg = sb("qkg", [128, 2], F32)
    hgg = sb("hgg", [128, 1], F32)
    lbt = sb("lbt", [128, 2, 8], F32)
    lbv = sb("lbv", [128, 2, 8], F32)
    esink = sb("esink", [128, 8], F32)
    ropec = sb("ropec", [128, 1], F32)
    cvec = sb("cvec", [128, 4], F32)
    kT = sb("kT", [128, 5 * 128], BF16)
    Vpad = sb("Vpad", [128, 5, 2, 128], BF16)
    onespad = sb("onespad", [128, 2, 128], BF16)
    S32 = sb("S32", [128, 8, 128], F32)
    Sbf = sb("Sbf", [128, 8, 128], BF16)
    KtT = sb("KtT", [128, 4, 128], BF16)
    Qh = sb("Qh", [128, TMAX], BF16)
    Kt = sb("Kt", [128, TMAX], BF16)
    ATm = sb("ATm", [128, 4, 128], BF16)
    Sring = sb("Sring", [128, 2, 2, 128], BF16)
    KtT2 = sb("KtT2", [128, 4, 128], BF16)
    Qh2 = sb("Qh2", [128, TMAX], BF16)
    Kt2 = sb("Kt2", [128, TMAX], BF16)
    ATm2 = sb("ATm2", [128, 4, 128], BF16)
    Kh = [sb("Khat0", [128, TMAX], BF16), sb("Khat1", [128, TMAX], BF16)]

    PS = [nc.alloc_psum_tensor("psd%d" % i, [128, 1024], F32) for i in range(4)]

    def bank(b):
        return PS[b // 2][:, (b % 2) * 512:(b % 2) * 512 + 512]

    BK = [Buf(bank(b), [("ps", b)]) for b in range(8)]

    X = [Buf(x32[:, c, :], [("x", c)]) for c in range(KC)]
    H = [Buf(hbf[:, c, :], [("h", c)]) for c in range(KC)]
    Hall = Buf(None, [("h", c) for c in range(KC)])
    A = [Buf(act[:, j, :], [("a", j)]) for j in range(FC)]
    WS = [Buf(wsl[i][:], [("w", i)]) for i in range(NSLOT)]
    FT = [Buf(Ft[i][:], [("f", i)]) for i in range(3)]
    XH = [Buf(x32h[:, c, :], [("xh", c)]) for c in range(KC)]
    CONST = Buf(None, ["const"])
    COS = Buf(cosT[:], ["cos"])
    SIN = Buf(sinT[:], ["sin"])
    POSI = Buf(posi[:], ["posi"])
    KTB = [Buf(kT[:, b * 128:(b + 1) * 128], [("kT", b)]) for b in range(5)]
    VPB = [Buf(Vpad[:, b, :, :], [("vp", b)]) for b in range(5)]
    SB32 = [Buf(S32[:, h, :], [("s32", h)]) for h in range(8)]
    SBF = [Buf(Sbf[:, h, :], [("sbf", h)]) for h in range(8)]
    KTT = [Buf(KtT[:, b, :], [("ktt", b)]) for b in range(4)]
    QH = Buf(Qh[:], ["qh"])
    KT_ = Buf(Kt[:], ["kt"])
    ATB = [Buf(ATm[:, b, :], [("atm", b)]) for b in range(4)]
    SRB = [[Buf(Sring[:, p_, r_, :], [("sring", p_, r_)]) for r_ in range(2)] for p_ in range(2)]
    HGSET = [
        dict(KtT=KtT, KTT=KTT, Qh=Qh, QH=QH, Kt=Kt, KT=KT_, ATm=ATm, ATB=ATB),
        dict(KtT=KtT2, KTT=[Buf(KtT2[:, b, :], [("ktt2", b)]) for b in range(4)], Qh=Qh2, QH=Buf(Qh2[:], ["qh2"]),
             Kt=Kt2, KT=Buf(Kt2[:], ["kt2"]), ATm=ATm2, ATB=[Buf(ATm2[:, b, :], [("atm2", b)]) for b in range(4)]),
    ]

    def f32view(j):
        v = act[:, j:j + 2, :].bitcast(F32).rearrange("p a t -> p (a t)")
        return Buf(v, [("a", j), ("a", j + 1)])

    HT = [f32view(32 + 2 * i) for i in range(6)]
    HTSET = [HT[0:5], [HT[5]] + [f32view(2 * i) for i in range(4)]]
    Eviews = [act[:, 40:44, :].rearrange("p (g k) t -> p g (k t)", g=2),
              act[:, 36:40, :].rearrange("p (g k) t -> p g (k t)", g=2)]
    EBS = [[Buf(Eviews[0][:, g, :], [("a", 40 + 2 * g), ("a", 41 + 2 * g)]) for g in range(2)],
           [Buf(Eviews[1][:, g, :], [("a", 36 + 2 * g), ("a", 37 + 2 * g)]) for g in range(2)]]
    pstrs = [PS[2][:, 0:512].bitcast(BF16), PS[2][:, 512:1024].bitcast(BF16)]
    rot_state = {"sc": 0, "pj": 0}
    vhv = act[:, 16:24, :].rearrange("p (b a) t -> p b (a t)", b=4)
    VH = [Buf(vhv[:, b, :], [("a", 16 + 2 * b), ("a", 17 + 2 * b)]) for b in range(4)]
    pstr = PS[3][:, 512:1024].bitcast(BF16)

    ident_bf = cmbf[:, 0, :]
    ones_bf = cmbf[:, 1, :]
    hgmask_bf = cmbf[:, 3, :]
    rot32 = cm32[:, 2, :]
    hblk32 = cm32[:, 4, :]
    ones32 = cm32[:, 1, :]

    def ld(dst, src, eng="sp", sem="c0"):
        S.add(eng, lambda e: e.dma_start(out=dst, in_=src), writes=[CONST], dma=sem)

    ld(cm32[:], cmat_d)
    ld(rmask[:], rmask_d)
    ld(gains[:], gains_d)
    ld(qkg[:], qkg_d)
    ld(hgg[:], hgg_d)
    ld(lbt[:], lbt_d)
    ld(ropec[:], ropec_d)
    ld(esink[0:64, :], sinks_d[:, 0:8].partition_broadcast(64))
    ld(esink[64:128, :], sinks_d[:, 8:16].partition_broadcast(64))
    ld(cmbf[:], cmat_d, eng="pool", sem="c1")
    ld(amask[:], amask_d, eng="pool", sem="c1")
    C0 = Buf(None, ["const"])
    S.add("dve", lambda e: e.memset(cvec[:, 0:1], EPS), writes=["cv0"])
    S.add("dve", lambda e: e.memset(cvec[:, 1:2], float(np.pi / 2)), writes=["cv1"])
    S.add("dve", lambda e: e.memset(S32[:], 0.0), writes=SB32)
    S.add("dve", lambda e: e.memset(Sbf[:], 0.0), writes=SBF)
    S.add("dve", lambda e: e.memset(Vpad[:], 0.0), writes=VPB)
    S.add("dve", lambda e: e.memset(kT[:], 0.0), writes=KTB)
    S.add("dve", lambda e: e.memset(onespad[:], 0.0), writes=["onespad"])
    S.add("dve", lambda e: e.memset(onespad[:, 0, 0:64], 1.0), reads=[], writes=["onespad"])
    S.add("dve", lambda e: e.memset(onespad[:, 1, 64:128], 1.0), reads=[], writes=["onespad"])
    S.add("dve", lambda e: e.tensor_tensor(out=lbv[:, 1, :], in0=lbt[:, 0, :], in1=lbt[:, 1, :], op=ALU.subtract),
          reads=[C0], writes=["lbv1"])
    S.add("act", lambda e: e.activation(out=lbv[:, 0, :], in_=lbv[:, 1, :], func=AF.Sigmoid),
          reads=["lbv1"], writes=["lbv0"])
    S.add("dve", lambda e: e.tensor_scalar(out=lbv[:, 1, :], in0=lbv[:, 0, :], scalar1=-1.0, scalar2=1.0,
                                           op0=ALU.mult, op1=ALU.add), reads=["lbv0"], writes=["lbv1"])
    S.add("act", lambda e: e.activation(out=esink[:], in_=esink[:], func=AF.Exp), reads=[C0], writes=["esink"])
    LBV = Buf(None, ["lbv0", "lbv1"])
    CV = Buf(None, ["cv0", "cv1"])

    stages = []

    def view3(base, a, b):
        return base[:, 0:a * b].rearrange("p (a b) -> p a b", a=a)

    def wview3(slot, a, b):
        return view3(wsl[slot], a, b)

    def tile_prog(ti, t0, T, halo, part):
        nblk = T // 128
        xb = x32h if halo else x32
        XK = XH if halo else X
        o0 = t0 - HALO

        has_next = (ti + 1 < n_tiles)
        tn0 = t0 + T

        def load_x_chunk(c, c0, Tn, q="sp"):
            S.add(q, lambda e: e.dma_start(out=xb[:, c, 0:Tn], in_=xT[c * 128:(c + 1) * 128, c0:c0 + Tn]),
                  writes=[XK[c]], dma=("xld", c))

        def load_pos(c0, Tn, q="sp"):
            S.add(q, lambda e: e.dma_start(out=posi[:, 0:Tn], in_=pos_d[:, c0:c0 + Tn].partition_broadcast(128)),
                  writes=[POSI], dma="pld")

        if ti <= 1 and "A" in part:
            def st_load(_):
                for c in range(KC):
                    load_x_chunk(c, t0, T)
                if halo:
                    load_pos(t0, T)
            stages.append((None, st_load))

        def norm(gi):
            def fn(_):
                for c in range(KC):
                    r = c % 2
                    S.add("act", lambda e, c=c, r=r: e.activation(out=act[:, 42 + r, 0:T], in_=xb[:, c, 0:T], func=AF.Square),
                          reads=[XK[c]], writes=[A[42 + r]])
                    S.add("pe", lambda e, c=c, r=r: e.matmul(bank(0)[:, 0:T], ones_bf, act[:, 42 + r, 0:T],
                                                             start=(c == 0), stop=(c == KC - 1)),
                          reads=[A[42 + r], CONST], writes=[BK[0]])
                S.add("act", lambda e: e.activation(out=Ft[2][:, 0:T], in_=bank(0)[:, 0:T], func=AF.Ln,
                                                    scale=1.0 / D, bias=cvec[:, 0:1]),
                      reads=[BK[0], CV], writes=[FT[2]])
                S.add("act", lambda e: e.activation(out=Ft[2][:, 0:T], in_=Ft[2][:, 0:T], func=AF.Exp, scale=-0.5),
                      reads=[FT[2]], writes=[FT[2]])
                for c in range(KC):
                    q = "dve"
                    S.add(q, lambda e, c=c: e.scalar_tensor_tensor(out=hbf[:, c, 0:T], in0=xb[:, c, 0:T],
                                                                   scalar=gains[:, gi, c:c + 1], in1=Ft[2][:, 0:T],
                                                                   op0=ALU.mult, op1=ALU.mult),
                          reads=[XK[c], FT[2], CONST], writes=[H[c]])
            return fn

        def ffn(which):
            stages.append((None, norm(0 if which == 0 else 2)))
            wg = wgu[which]
            wd = wdn[which]
            wg3 = wg.rearrange("(c p) n -> p c n", p=128)
            for jp in range(FC // 2):
                pieces = [((KC, 512, 0, 256), wg3[:, :, jp * 256:jp * 256 + 256]),
                          ((KC, 512, 256, 512), wg3[:, :, DFF + jp * 256:DFF + jp * 256 + 256])]

                def fn(slot, jp=jp):
                    w3 = wview3(slot, KC, 512)
                    if jp == 0 and T == 512:
                        for kc in range(KC):
                            def mmk(e, kc=kc):
                                ins = None
                                for i in range(2):
                                    for (bb, off) in ((2 * i, i * 128), (2 * i + 1, 256 + i * 128)):
                                        ins = e.matmul(bank(bb)[:, 0:T], w3[:, kc, off:off + 128], hbf[:, kc, 0:T],
                                                       start=(kc == 0), stop=(kc == KC - 1))
                                return ins
                            S.add("pe", mmk, reads=[WS[slot], H[kc]], writes=[BK[0], BK[1], BK[2], BK[3]])
                    for i in range(2):
                        j = 2 * jp + i
                        bg = (jp % 2) * 4 + 2 * i
                        bu = bg + 1
                        for (bb, off) in ((bg, i * 128), (bu, 256 + i * 128)):
                            if jp == 0 and T == 512:
                                continue
                            def mm(e, bb=bb, off=off):
                                ins = None
                                for kc in range(KC):
                                    ins = e.matmul(bank(bb)[:, 0:T], w3[:, kc, off:off + 128], hbf[:, kc, 0:T],
                                                   start=(kc == 0), stop=(kc == KC - 1))
                                return ins
                            S.add("pe", mm, reads=[WS[slot], Hall], writes=[BK[bb]])
                        fi = i
                        S.add("act", lambda e, bg=bg, fi=fi: e.activation(out=Ft[fi][:, 0:T], in_=bank(bg)[:, 0:T], func=AF.Silu),
                              reads=[BK[bg]], writes=[FT[fi]])
                        S.add("dve", lambda e, bu=bu, fi=fi, j=j: e.tensor_tensor(out=act[:, j, 0:T], in0=Ft[fi][:, 0:T],
                                                                                in1=bank(bu)[:, 0:T], op=ALU.mult),
                              reads=[FT[fi], BK[bu]], writes=[A[j]])
                stages.append((pieces, fn, ("gu", which, jp)))
            wd3 = wd.rearrange("(k p) n -> p k n", p=128)
            for mg in range(8):
                for kh in range(2):
                    pieces = [((22, 256, 0, 256), wd3[:, kh * 22:(kh + 1) * 22, mg * 256:(mg + 1) * 256])]

                    def fn(slot, mg=mg, kh=kh):
                        w3 = wview3(slot, 22, 256)
                        for mi in range(2):
                            bb = (mg % 2) * 2 + mi
                            m = mg * 2 + mi

                            def mm(e, bb=bb, mi=mi):
                                ins = None
                                for k in range(22):
                                    ins = e.matmul(bank(bb)[:, 0:T], w3[:, k, mi * 128:(mi + 1) * 128], act[:, kh * 22 + k, 0:T],
                                                   start=(kh == 0 and k == 0), stop=(kh == 1 and k == 21))
                                return ins
                            S.add("pe", mm, reads=[WS[slot]] + A[kh * 22:(kh + 1) * 22], writes=[BK[bb]])
                            if kh == 1:
                                S.add("dve", lambda e, bb=bb, m=m: e.scalar_tensor_tensor(
                                    out=xb[:, m, 0:T], in0=bank(bb)[:, 0:T], scalar=0.5, in1=xb[:, m, 0:T],
                                    op0=ALU.mult, op1=ALU.add), reads=[BK[bb], XK[m]], writes=[XK[m]])
                                if which == 1:
                                    S.add("pool", lambda e, m=m: e.dma_start(out=outT[m * 128:(m + 1) * 128, o0:o0 + T], in_=xb[:, m, 0:T]),
                                          reads=[XK[m]], writes=[("outdram", m)], dma=("ost", m))
                                    if has_next:
                                        load_x_chunk(m, tn0, 512, "pool")
                                        if m == KC - 1:
                                            load_pos(tn0, 512, "pool")
                    stages.append((pieces, fn, ("dn", which, mg, kh)))

        if "A" in part:
            ffn(0)
        if "B" not in part:
            return
        if ti == 1:
            stages.append((None, lambda _: load_pos(t0, T)))
        stages.append((None, norm(1)))

        def st_rope(_):
            a = HT[0].ap
            k_ = HT[1].ap
            AB = HT[0]
            KB = HT[1]
            S.add("dve", lambda e: e.tensor_copy(out=a[:, 0:T], in_=posi[:, 0:T]), reads=[POSI], writes=[AB])
            S.add("dve", lambda e: e.tensor_scalar(out=a[:, 0:T], in0=a[:, 0:T], scalar1=ropec[:, 0:1], scalar2=None,
                                                   op0=ALU.mult), reads=[AB, CONST], writes=[AB])
            S.add("dve", lambda e: e.tensor_scalar(out=k_[:, 0:T], in0=a[:, 0:T], scalar1=float(1.0 / (2 * np.pi)),
                                                   scalar2=12582912.0, op0=ALU.mult, op1=ALU.add), reads=[AB], writes=[KB])
            S.add("dve", lambda e: e.tensor_scalar(out=k_[:, 0:T], in0=k_[:, 0:T], scalar1=12582912.0, scalar2=None,
                                                   op0=ALU.subtract), reads=[KB], writes=[KB])
            S.add("dve", lambda e: e.scalar_tensor_tensor(out=a[:, 0:T], in0=k_[:, 0:T], scalar=-TWO_PI_HI, in1=a[:, 0:T],
                                                          op0=ALU.mult, op1=ALU.add), reads=[KB, AB], writes=[AB])
            S.add("dve", lambda e: e.scalar_tensor_tensor(out=a[:, 0:T], in0=k_[:, 0:T], scalar=-float(TWO_PI_LO), in1=a[:, 0:T],
                                                          op0=ALU.mult, op1=ALU.add), reads=[KB, AB], writes=[AB])
            S.add("dve", lambda e: e.tensor_scalar(out=a[:, 0:T], in0=a[:, 0:T], scalar1=-PI_CLAMP, scalar2=PI_CLAMP,
                                                   op0=ALU.max, op1=ALU.min), reads=[AB], writes=[AB])
            S.add("act", lambda e: e.activation(out=sinT[:, 0:T], in_=a[:, 0:T], func=AF.Sin), reads=[AB], writes=[SIN])
            S.add("dve", lambda e: e.scalar_tensor_tensor(out=k_[:, 0:T], in0=a[:, 0:T], scalar=-1.0, in1=a[:, 0:T], op0=ALU.mult, op1=ALU.max),
                  reads=[AB], writes=[KB])
            S.add("act", lambda e: e.activation(out=cosT[:, 0:T], in_=k_[:, 0:T], func=AF.Sin, scale=-1.0, bias=cvec[:, 1:2]),
                  reads=[KB, CV], writes=[COS])
        stages.append((None, st_rope))

        win3 = win.rearrange("(c p) n -> p c n", p=128)

        def proj(slot, w3, off, bb):
            def mm(e):
                ins = None
                for kc in range(KC):
                    ins = e.matmul(bank(bb)[:, 0:T], w3[:, kc, off:off + 128], hbf[:, kc, 0:T],
                                   start=(kc == 0), stop=(kc == KC - 1))
                return ins
            S.add("pe", mm, reads=[WS[slot], Hall], writes=[BK[bb]])

        def qk_chain(slot, w3, off, bb, gcol, dst_ap, dst_buf, par, preproj=False):
            if par == 0:
                zb, q2b = FT[0], FT[1]
                z, q2 = Ft[0], Ft[1]
            else:
                zb, q2b = HT[0], HT[1]
                z, q2 = HT[0].ap, HT[1].ap
            sb2 = 4 + (bb % 2)
            if not preproj:
                proj(slot, w3, off, bb)
                yield
            S.add("act", lambda e: e.activation(out=z[:, 0:T], in_=bank(bb)[:, 0:T], func=AF.Copy), reads=[BK[bb]], writes=[zb])
            yield
            S.add("act", lambda e: e.activation(out=q2[:, 0:T], in_=bank(bb)[:, 0:T], func=AF.Square), reads=[BK[bb]], writes=[q2b])
            yield
            S.add("pe", lambda e: e.matmul(bank(sb2)[:, 0:T], hblk32, q2[:, 0:T], start=True, stop=True),
                  reads=[q2b, CONST], writes=[BK[sb2]])
            yield
            S.add("act", lambda e: e.activation(out=q2[:, 0:T], in_=bank(sb2)[:, 0:T], func=AF.Ln, scale=1.0 / 64, bias=cvec[:, 0:1]),
                  reads=[BK[sb2], CV], writes=[q2b])
            yield
            S.add("act", lambda e: e.activation(out=q2[:, 0:T], in_=q2[:, 0:T], func=AF.Exp, scale=-0.5), reads=[q2b], writes=[q2b])
            yield
            S.add("dve", lambda e: e.scalar_tensor_tensor(out=z[:, 0:T], in0=z[:, 0:T], scalar=qkg[:, gcol:gcol + 1], in1=q2[:, 0:T],
                                                          op0=ALU.mult, op1=ALU.mult), reads=[zb, q2b, CONST], writes=[zb])
            yield
            S.add("pe", lambda e: e.matmul(bank(sb2 + 2)[:, 0:T], rot32, z[:, 0:T], start=True, stop=True),
                  reads=[zb, CONST], writes=[BK[sb2 + 2]])
            yield
            S.add("dve", lambda e: e.tensor_tensor(out=q2[:, 0:T], in0=bank(sb2 + 2)[:, 0:T], in1=sinT[:, 0:T], op=ALU.mult),
                  reads=[BK[sb2 + 2], SIN], writes=[q2b])
            yield
            S.add("dve", lambda e: e.tensor_tensor(out=z[:, 0:T], in0=z[:, 0:T], in1=cosT[:, 0:T], op=ALU.mult),
                  reads=[zb, COS], writes=[zb])
            yield
            S.add("dve", lambda e: e.tensor_tensor(out=dst_ap, in0=z[:, 0:T], in1=q2[:, 0:T], op=ALU.add),
                  reads=[zb, q2b], writes=[dst_buf])
            yield

        if not halo:
            wq5 = win[:, OQ:OQ + 1024].rearrange("(kc p) (g c d) -> p kc c g d", p=128, g=2, c=8)
            for qg in range(2):
                pieces = []
                for ci in range(4):
                    for g in range(2):
                        pieces.append(((KC, 512, ci * 128 + g * 64, ci * 128 + g * 64 + 64), wq5[:, :, qg * 4 + ci, g, :]))

                def fn(slot, qg=qg):
                    w3 = wview3(slot, KC, 512)
                    pre = (qg == 0)
                    if pre:
                        for kc in range(KC):
                            def mmk(e, kc=kc):
                                ins = None
                                for ci in range(2):
                                    ins = e.matmul(bank(ci)[:, 0:T], w3[:, kc, ci * 128:(ci + 1) * 128], hbf[:, kc, 0:T],
                                                   start=(kc == 0), stop=(kc == KC - 1))
                                return ins
                            S.add("pe", mmk, reads=[WS[slot], H[kc]], writes=[BK[0], BK[1]])
                    pipeline([qk_chain(slot, w3, ci * 128, ci % 2, 0, act[:, qg * 4 + ci, 0:T], A[qg * 4 + ci], ci % 2,
                                       preproj=(pre and ci < 2))
                              for ci in range(4)], depth=2, offset=3)
                stages.append((pieces, fn, ("q", qg)))

        pieces = [((KC, 256, 0, 256), win3[:, :, OK_:OK_ + 256])]

        def fn_kv(slot):
            w3 = wview3(slot, KC, 256)
            kdst = kT[:, 128:128 + T]
            for _ in qk_chain(slot, w3, 0, 0, 1, kdst, Buf(None, [("kT", 1 + b) for b in range(nblk)]), 0):
                pass
            for b in range(nblk):
                bb = 2 + (b % 2)

                def mm(e, b=b, bb=bb):
                    ins = None
                    for kc in range(KC):
                        ins = e.matmul(bank(bb)[:, 0:128], hbf[:, kc, b * 128:(b + 1) * 128], w3[:, kc, 128:256],
                                       start=(kc == 0), stop=(kc == KC - 1))
                    return ins
                S.add("pe", mm, reads=[WS[slot], Hall], writes=[BK[bb]])
                S.add("act", lambda e, b=b, bb=bb: e.activation(out=Vpad[:, 1 + b, 0, 0:64], in_=bank(bb)[:, 0:64], func=AF.Copy),
                      reads=[BK[bb]], writes=[VPB[1 + b]])
                S.add("act", lambda e, b=b, bb=bb: e.activation(out=Vpad[:, 1 + b, 1, 64:128], in_=bank(bb)[:, 64:128], func=AF.Copy),
                      reads=[BK[bb]], writes=[VPB[1 + b]])
        stages.append((pieces, fn_kv, ("kv",)))

        if not halo:
            def attn_unit(qb, u, uidx):
                first = (ti == 1 and qb == 0)
                mk = amask[:, 1 if first else 0, :]
                es = uidx % 2
                Ev = Eviews[es]
                EBx = EBS[es]
                denb = FT[uidx % 2]
                den = Ft[uidx % 2]
                for g in range(2):
                    d2 = (0, 1, 3)[rot_state["sc"] % 3]
                    rot_state["sc"] += 1
                    PSd = PS[d2]
                    bks = [BK[2 * d2], BK[2 * d2 + 1]]

                    def mm(e, g=g, PSd=PSd):
                        ins = None
                        for kb in range(2):
                            for ci in range(4):
                                cp = u * 4 + ci
                                ins = e.matmul(PSd[:, kb * 512 + ci * 128: kb * 512 + ci * 128 + 128],
                                               kT[g * 64:(g + 1) * 64, (qb + kb) * 128:(qb + kb + 1) * 128],
                                               act[g * 64:(g + 1) * 64, cp, qb * 128:(qb + 1) * 128],
                                               start=True, stop=True)
                        return ins
                    S.add("pe", mm, reads=[KTB[qb], KTB[qb + 1]] + A[u * 4:u * 4 + 4], writes=bks)
                    yield
                    S.add("act", lambda e, g=g, PSd=PSd: e.activation(out=Ev[:, g, :], in_=PSd[:, :], func=AF.Exp, scale=0.125),
                          reads=bks, writes=[EBx[g]])
                    yield
                    S.add("dve", lambda e, g=g: e.tensor_tensor(out=Ev[:, g, :], in0=Ev[:, g, :], in1=mk, op=ALU.mult),
                          reads=[EBx[g], CONST], writes=[EBx[g]])
                    yield

                def pv(e):
                    ins = None
                    n = 0
                    for g in range(2):
                        for kb in range(2):
                            ins = e.matmul(bank(4)[:, :], Vpad[:, qb + kb, g, :], Ev[:, g, kb * 512:(kb + 1) * 512],
                                           start=(n == 0), stop=(n == 3))
                            n += 1
                    n = 0
                    for g in range(2):
                        for kb in range(2):
                            ins = e.matmul(bank(5)[:, :], onespad[:, g, :], Ev[:, g, kb * 512:(kb + 1) * 512],
                                           start=(n == 0), stop=(n == 3))
                            n += 1
                    return ins
                S.add("pe", pv, reads=[EBx[0], EBx[1], VPB[qb], VPB[qb + 1], "onespad"], writes=[BK[4], BK[5]])
                yield
                den3 = den[:, :].rearrange("p (c q) -> p c q", c=4)
                S.add("dve", lambda e: e.tensor_tensor(
                    out=den3, in0=bank(5)[:, :].rearrange("p (c q) -> p c q", c=4),
                    in1=esink[:, u * 4:u * 4 + 4].unsqueeze(2).to_broadcast([128, 4, 128]), op=ALU.add),
                    reads=[BK[5], "esink"], writes=[denb])
                yield
                S.add("act", lambda e: e.activation(out=den[:, :], in_=den[:, :], func=AF.Ln), reads=[denb], writes=[denb])
                yield
                S.add("act", lambda e: e.activation(out=den[:, :], in_=den[:, :], func=AF.Exp, scale=-1.0), reads=[denb], writes=[denb])
                yield
                S.add("dve", lambda e: e.tensor_tensor(
                    out=act[:, 8 + u * 4:8 + u * 4 + 4, qb * 128:(qb + 1) * 128],
                    in0=bank(4)[:, :].rearrange("p (c q) -> p c q", c=4), in1=den3, op=ALU.mult),
                    reads=[BK[4], denb], writes=A[8 + u * 4:8 + u * 4 + 4])
                yield

            def st_attn(_):
                units = []
                n = 0
                for qb in range(nblk):
                    for u in range(2):
                        units.append(attn_unit(qb, u, n))
                        n += 1
                pipeline(units, depth=2, offset=4)
            stages.append((None, st_attn))

        def st_shift(_):
            S.add("dve", lambda e: e.tensor_copy(out=kT[:, 0:128], in_=kT[:, nblk * 128:(nblk + 1) * 128]),
                  reads=[KTB[nblk]], writes=[KTB[0]])
            S.add("dve", lambda e: e.tensor_copy(out=Vpad[:, 0, :, :], in_=Vpad[:, nblk, :, :]),
                  reads=[VPB[nblk]], writes=[VPB[0]])
        stages.append((None, st_shift))

        for half in range(2):
            pieces = [((KC, 512, 0, 512), win3[:, :, OGI + half * 512:OGI + (half + 1) * 512])]

            def fn_gi(slot, half=half):
                w3 = wview3(slot, KC, 512)
                for b in range(nblk):
                    bb = (b % 2)

                    def mm(e, b=b, bb=bb):
                        ins = None
                        for kc in range(KC):
                            ins = e.matmul(bank(bb)[:, :], hbf[:, kc, b * 128:(b + 1) * 128], w3[:, kc, :],
                                           start=(kc == 0), stop=(kc == KC - 1))
                        return ins
                    S.add("pe", mm, reads=[WS[slot], Hall], writes=[BK[bb]])
                    S.add("act", lambda e, b=b, bb=bb: e.activation(out=vhv[:, b, half * 512:(half + 1) * 512], in_=bank(bb)[:, :], func=AF.Copy),
                          reads=[BK[bb]], writes=[VH[b]])
            stages.append((pieces, fn_gi, ("gi", half)))

        for hd in range(8):
            pieces = [((KC, 384, 0, 128), win3[:, :, OGF + hd * 128:OGF + (hd + 1) * 128]),
                      ((KC, 384, 128, 256), win3[:, :, OGQ + hd * 128:OGQ + (hd + 1) * 128]),
                      ((KC, 384, 256, 384), win3[:, :, OGO + hd * 128:OGO + (hd + 1) * 128])]

            def fn_hg(slot, hd=hd):
                par = hd % 2
                R_ = HGSET[par]
                KtTx, KTTx, Qhx, QHx, Ktx, KTx, ATmx, ATBx = (R_["KtT"], R_["KTT"], R_["Qh"], R_["QH"], R_["Kt"],
                                                              R_["KT"], R_["ATm"], R_["ATB"])
                w3 = wview3(slot, KC, 384)
                tf, tl, tb, teb, tq = HTSET[par]
                tenb = tl
                ob = 3 - par
                ab = 4 + par
                ub = 6 + par
                pstr_ = pstrs[par]
                Khx = Kh[par]
                KHx = Buf(Khx[:], [("khat", par)])

                def pbank():
                    b = rot_state["pj"] % 2
                    rot_state["pj"] += 1
                    return b
                pb = pbank()
                proj(slot, w3, 0, pb)
                yield
                if not halo:
                    pb2 = pbank()
                    proj(slot, w3, 128, pb2)
                    yield
                S.add("act", lambda e: e.activation(out=tf.ap[:, 0:T], in_=bank(pb)[:, 0:T], func=AF.Sigmoid), reads=[BK[pb]], writes=[tf])
                yield
                if not halo:
                    S.add("act", lambda e: e.activation(out=tq.ap[:, 0:T], in_=bank(pb2)[:, 0:T], func=AF.Sigmoid), reads=[BK[pb2]], writes=[tq])
                    yield
                    S.add("dve", lambda e: e.tensor_tensor(out=tq.ap[:, 0:T], in0=tq.ap[:, 0:T], in1=bank(pb2)[:, 0:T], op=ALU.mult),
                          reads=[tq, BK[pb2]], writes=[tq])
                    yield
                    pb3 = pbank()
                    proj(slot, w3, 256, pb3)
                    yield
                    S.add("act", lambda e: e.activation(out=tb.ap[:, 0:T], in_=bank(pb3)[:, 0:T], func=AF.Sigmoid), reads=[BK[pb3]], writes=[tb])
                    yield
                    S.add("dve", lambda e: e.tensor_tensor(out=act[:, 24 + hd, 0:T], in0=tb.ap[:, 0:T], in1=bank(pb3)[:, 0:T], op=ALU.mult),
                          reads=[tb, BK[pb3]], writes=[A[24 + hd]])
                    yield
                S.add("dve", lambda e: e.tensor_scalar(out=tf.ap[:, 0:T], in0=tf.ap[:, 0:T], scalar1=lbv[:, 1, hd:hd + 1],
                                                       scalar2=lbv[:, 0, hd:hd + 1], op0=ALU.mult, op1=ALU.add),
                      reads=[tf, LBV], writes=[tf])
                yield
                S.add("act", lambda e: e.activation(out=tl.ap[:, 0:T], in_=tf.ap[:, 0:T], func=AF.Ln), reads=[tf], writes=[tl])
                yield
                S.add("dve", lambda e: e.tensor_tensor_scan(out=tb.ap[:, 0:T], data0=rmask[:, 0:T], data1=tl.ap[:, 0:T], initial=0.0,
                                                            op0=ALU.mult, op1=ALU.add), reads=[tl, CONST], writes=[tb])
                yield
                S.add("act", lambda e: e.activation(out=teb.ap[:, 0:T], in_=tb.ap[:, 0:T], func=AF.Exp), reads=[tb], writes=[teb])
                yield
                S.add("act", lambda e: e.activation(out=tenb.ap[:, 0:T], in_=tb.ap[:, 0:T], func=AF.Exp, scale=-1.0), reads=[tb], writes=[tenb])
                yield
                S.add("dve", lambda e: e.tensor_scalar(out=tf.ap[:, 0:T], in0=tf.ap[:, 0:T], scalar1=-1.0, scalar2=1.0,
                                                       op0=ALU.mult, op1=ALU.add), reads=[tf], writes=[tf])
                yield
                S.add("dve", lambda e: e.tensor_tensor(out=Ktx[:, 0:T], in0=tf.ap[:, 0:T], in1=tenb.ap[:, 0:T], op=ALU.mult),
                      reads=[tf, tenb], writes=[KTx])
                yield
                if not halo:
                    S.add("dve", lambda e: e.tensor_tensor(out=Qhx[:, 0:T], in0=tq.ap[:, 0:T], in1=teb.ap[:, 0:T], op=ALU.mult),
                          reads=[tq, teb], writes=[QHx])
                    yield
                nch = T // 64
                S.add("dve", lambda e: e.tensor_tensor(
                    out=Khx[:, 0:T].rearrange("p (c t) -> p c t", t=64), in0=Ktx[:, 0:T].rearrange("p (c t) -> p c t", t=64),
                    in1=teb.ap[:, 0:T].rearrange("p (c t) -> p c t", t=64)[:, :, 63:64].to_broadcast([128, nch, 64]), op=ALU.mult),
                    reads=[KTx, teb], writes=[KHx])
                yield
                for b in range(nblk):
                    S.add("pe", lambda e, b=b: e.transpose(pstr_[:, b * 128:(b + 1) * 128], Khx[:, b * 128:(b + 1) * 128], ident_bf),
                          reads=[KHx, CONST], writes=[BK[ab]])
                yield
                S.add("act", lambda e: e.activation(out=KtTx[:, 0:nblk, :], in_=pstr_[:, 0:nblk * 128].rearrange("p (b d) -> p b d", b=nblk),
                                                    func=AF.Copy), reads=[BK[ab]], writes=KTTx[0:nblk])
                yield
                nchk = 2 * nblk

                def u_mm(c):
                    b_, c2_ = divmod(c, 2)
                    ubc = ub if c % 2 == 0 else ab
                    S.add("pe", lambda e: e.matmul(bank(ubc)[:, 0:128], KtTx[c2_ * 64:(c2_ + 1) * 64, b_, :],
                                                   vhv[c2_ * 64:(c2_ + 1) * 64, b_, hd * 128:(hd + 1) * 128],
                                                   start=True, stop=True), reads=[KTTx[b_], VH[b_]], writes=[BK[ubc]])

                def s_in(c):
                    if c == 0:
                        return Sbf[:, hd, :], SBF[hd]
                    return Sring[:, par, (c - 1) % 2, :], SRB[par][(c - 1) % 2]

                for c in range(min(1, nchk)):
                    u_mm(c)
                yield
                for c in range(nchk):
                    b = c // 2
                    col = c * 64
                    ubc = ub if c % 2 == 0 else ab
                    if c % 2 == 1 and c + 1 < nchk:
                        u_mm(c + 1)
                        yield
                    if c % 2 == 0 and not halo:
                        S.add("pe", lambda e, b=b: e.matmul(bank(ab)[:, 0:128], Ktx[:, b * 128:(b + 1) * 128], Qhx[:, b * 128:(b + 1) * 128],
                                                            start=True, stop=True), reads=[KTx, QHx], writes=[BK[ab]])
                        yield
                        S.add("dve", lambda e, b=b: e.tensor_tensor(out=ATmx[:, b, :], in0=bank(ab)[:, 0:128], in1=hgmask_bf, op=ALU.mult),
                              reads=[BK[ab], CONST], writes=[ATBx[b]])
                        yield
                    if c % 2 == 0 and c + 1 < nchk:
                        u_mm(c + 1)
                        yield
                    if c % 2 == 0 and not halo:
                        S.add("pe", lambda e, b=b: e.matmul(bank(ob)[:, b * 128:(b + 1) * 128], vhv[:, b, hd * 128:(hd + 1) * 128], ATmx[:, b, :],
                                                            start=True, stop=False), reads=[VH[b], ATBx[b]], writes=[BK[ob]])
                        yield
                    if not halo:
                        sap, sbuf_ = s_in(c)
                        S.add("pe", lambda e, col=col, c=c, sap=sap: e.matmul(bank(ob)[:, col:col + 64], sap, Qhx[:, col:col + 64],
                                                                             start=False, stop=(c % 2 == 1)),
                              reads=[sbuf_, QHx], writes=[BK[ob]])
                        yield
                    if c == nchk - 1:
                        dap, dbuf = Sbf[:, hd, :], SBF[hd]
                    else:
                        dap, dbuf = Sring[:, par, c % 2, :], SRB[par][c % 2]
                    S.add("dve", lambda e, col=col, ubc=ubc, dap=dap: e.scalar_tensor_tensor(
                        out=dap, in0=S32[:, hd, :], scalar=teb.ap[:, col + 63:col + 64], in1=bank(ubc)[:, 0:128],
                        op0=ALU.mult, op1=ALU.add), reads=[BK[ubc], SB32[hd], teb], writes=[dbuf])
                    yield
                    S.add("dve", lambda e, col=col, ubc=ubc: e.scalar_tensor_tensor(
                        out=S32[:, hd, :], in0=S32[:, hd, :], scalar=teb.ap[:, col + 63:col + 64], in1=bank(ubc)[:, 0:128],
                        op0=ALU.mult, op1=ALU.add), reads=[BK[ubc], SB32[hd], teb], writes=[SB32[hd]])
                    yield
                if not halo:
                    o32, osq, lnv = tf, tl, tb
                    S.add("act", lambda e: e.activation(out=o32.ap[:, 0:T], in_=bank(ob)[:, 0:T], func=AF.Copy), reads=[BK[ob]], writes=[o32])
                    yield
                    S.add("act", lambda e: e.activation(out=osq.ap[:, 0:T], in_=bank(ob)[:, 0:T], func=AF.Square), reads=[BK[ob]], writes=[osq])
                    yield
                    pb4 = pbank()
                    S.add("pe", lambda e: e.matmul(bank(pb4)[:, 0:T], ones32, osq.ap[:, 0:T], start=True, stop=True),
                          reads=[osq, CONST], writes=[BK[pb4]])
                    yield
                    S.add("act", lambda e: e.activation(out=lnv.ap[:, 0:T], in_=bank(pb4)[:, 0:T], func=AF.Ln, scale=1.0 / 128, bias=cvec[:, 0:1]),
                          reads=[BK[pb4], CV], writes=[lnv])
                    yield
                    S.add("act", lambda e: e.activation(out=lnv.ap[:, 0:T], in_=lnv.ap[:, 0:T], func=AF.Exp, scale=-0.5), reads=[lnv], writes=[lnv])
                    yield
                    S.add("dve", lambda e: e.scalar_tensor_tensor(out=o32.ap[:, 0:T], in0=o32.ap[:, 0:T], scalar=hgg[:, 0:1], in1=lnv.ap[:, 0:T],
                                                                  op0=ALU.mult, op1=ALU.mult), reads=[o32, lnv, CONST], writes=[o32])
                    yield
                    S.add("dve", lambda e: e.tensor_tensor(out=act[:, 24 + hd, 0:T], in0=o32.ap[:, 0:T], in1=act[:, 24 + hd, 0:T], op=ALU.mult),
                          reads=[o32, A[24 + hd]], writes=[A[24 + hd]])
                    yield
            stages.append((pieces, fn_hg, ("hg", hd), "pipe_hg"))

        if halo:
            return

        wA4 = wA.rearrange("(g c d) n -> d g c n", g=2, c=8)
        wR3 = wR.rearrange("(c p) n -> p c n", p=128)

        def merged_slot(m):
            return m if m < 8 else 16 + (m - 8)

        for mp in range(8):
            def ar_dma(base, mp=mp):
                w3 = view3(base, 16, 256)
                res = []
                for g in range(2):
                    res.append((w3[g * 64:(g + 1) * 64, 0:8, :], wA4[:, g, :, mp * 256:(mp + 1) * 256]))
                res.append((w3[:, 8:16, :], wR3[:, :, mp * 256:(mp + 1) * 256]))
                return res
            arslot = {}

            def fn_ar(slot, arslot=arslot):
                arslot["s"] = slot
            stages.append((("raw", ar_dma), fn_ar, ("ar", mp)))
            piecesBR = [((KC, 512, 0, 256), win3[:, :, OBR + mp * 256:OBR + mp * 256 + 256]),
                        ((KC, 512, 256, 512), win3[:, :, OBR + D + mp * 256:OBR + D + mp * 256 + 256])]

            def fn_br(slot, mp=mp, arslot=arslot):
                sA = arslot["s"]
                wa3 = wview3(sA, 16, 256)
                w3 = wview3(slot, KC, 512)
                for mi in range(2):
                    m = mp * 2 + mi
                    off = mi * 128
                    b0 = (mi % 2) * 4
                    pAb, pRb, gAb, gRb = b0, b0 + 1, b0 + 2, b0 + 3

                    def mmA(e, off=off, pAb=pAb):
                        ins = None
                        for c in range(8):
                            ins = e.matmul(bank(pAb)[:, 0:T], wa3[:, c, off:off + 128], act[:, 8 + c, 0:T], start=(c == 0), stop=(c == 7))
                        return ins
                    S.add("pe", mmA, reads=[WS[sA]] + A[8:16], writes=[BK[pAb]])

                    def mmR(e, off=off, pRb=pRb):
                        ins = None
                        for c in range(8):
                            ins = e.matmul(bank(pRb)[:, 0:T], wa3[:, 8 + c, off:off + 128], act[:, 24 + c, 0:T], start=(c == 0), stop=(c == 7))
                        return ins
                    S.add("pe", mmR, reads=[WS[sA]] + A[24:32], writes=[BK[pRb]])
                    proj(slot, w3, mi * 128, gAb)
                    proj(slot, w3, 256 + mi * 128, gRb)
                    S.add("act", lambda e, gAb=gAb: e.activation(out=Ft[0][:, 0:T], in_=bank(gAb)[:, 0:T], func=AF.Sigmoid), reads=[BK[gAb]], writes=[FT[0]])
                    S.add("act", lambda e, gRb=gRb: e.activation(out=Ft[1][:, 0:T], in_=bank(gRb)[:, 0:T], func=AF.Sigmoid), reads=[BK[gRb]], writes=[FT[1]])
                    S.add("dve", lambda e, pAb=pAb: e.tensor_tensor(out=Ft[0][:, 0:T], in0=Ft[0][:, 0:T], in1=bank(pAb)[:, 0:T], op=ALU.mult),
                          reads=[FT[0], BK[pAb]], writes=[FT[0]])
                    S.add("dve", lambda e, pRb=pRb: e.tensor_tensor(out=Ft[1][:, 0:T], in0=Ft[1][:, 0:T], in1=bank(pRb)[:, 0:T], op=ALU.mult),
                          reads=[FT[1], BK[pRb]], writes=[FT[1]])
                    ms = merged_slot(m)
                    S.add("dve", lambda e, ms=ms: e.tensor_tensor(out=act[:, ms, 0:T], in0=Ft[0][:, 0:T], in1=Ft[1][:, 0:T], op=ALU.add),
                          reads=[FT[0], FT[1]], writes=[A[ms]])
            stages.append((piecesBR, fn_br, ("br", mp)))

        wO3 = wO.rearrange("(c p) n -> p c n", p=128)
        MERG = [A[merged_slot(m)] for m in range(16)]
        for mq in range(4):
            pieces = [((KC, 512, 0, 512), wO3[:, :, mq * 512:(mq + 1) * 512])]

            def fn_o(slot, mq=mq):
                w3 = wview3(slot, KC, 512)
                for mi in range(4):
                    m = mq * 4 + mi
                    bb = mi % 4

                    def mm(e, mi=mi, bb=bb):
                        ins = None
                        for kc in range(KC):
                            ins = e.matmul(bank(bb)[:, 0:T], w3[:, kc, mi * 128:(mi + 1) * 128], act[:, merged_slot(kc), 0:T],
                                           start=(kc == 0), stop=(kc == KC - 1))
                        return ins
                    S.add("pe", mm, reads=[WS[slot]] + MERG, writes=[BK[bb]])
                    S.add("dve", lambda e, m=m, bb=bb: e.tensor_tensor(out=xb[:, m, 0:T], in0=xb[:, m, 0:T], in1=bank(bb)[:, 0:T], op=ALU.add),
                          reads=[XK[m], BK[bb]], writes=[XK[m]])
            stages.append((pieces, fn_o, ("wo", mq)))

        ffn(1)

    tiles = [(0, 0, HALO, True)] + [(1 + i, HALO + 512 * i, 512, False) for i in range(8)]
    if n_tiles >= 2:
        tile_prog(*tiles[1], "A")
        tile_prog(*tiles[0], "AB")
        tile_prog(*tiles[1], "B")
        for tl_ in tiles[2:n_tiles]:
            tile_prog(*tl_, "AB")
    else:
        tile_prog(*tiles[0], "AB")

    def pieces_to_list(pieces, base):
        if isinstance(pieces, tuple) and pieces[0] == "raw":
            return pieces[1](base)
        return [(view3(base, a, b)[:, :, c0:c1], src) for (a, b, c0, c1), src in pieces]

    scratch = {}
    ncv = 0
    for st in stages:
        if st[0] is None:
            continue
        sid = st[2]
        if sid in scratch:
            continue
        scr = nc.dram_tensor("scr_" + "_".join(str(v) for v in sid), [128, SLOT_ELEMS], BF16).ap()
        scratch[sid] = scr
        tok = ncv % 8
        for dst, src in pieces_to_list(st[0], scr):
            S.add("pool", lambda e, dst=dst, src=src: e.dma_start(out=dst, in_=src),
                  writes=[("scr", sid), ("cvtok", tok)], dma=("cv", tok))
        ncv += 1

    wstages = [i for i, st in enumerate(stages) if st[0] is not None]
    slot_of = {si: n % NSLOT for n, si in enumerate(wstages)}
    planned = [0]

    def stage_elems(pieces):
        if isinstance(pieces, tuple) and pieces[0] == "raw":
            return 16 * 256
        return max(a * b for (a, b, c0, c1), src in pieces)

    def plan_dma(upto_n):
        while planned[0] < len(wstages) and planned[0] <= upto_n:
            si = wstages[planned[0]]
            slot = slot_of[si]
            sid = stages[si][2]
            ne = stage_elems(stages[si][0])
            S.add("sp", lambda e, slot=slot, sid=sid, ne=ne: e.dma_start(out=wsl[slot][:, 0:ne], in_=scratch[sid][:, 0:ne]),
                  reads=[("scr", sid)], writes=[WS[slot]], dma=("wsem", slot))
            planned[0] += 1

    widx = {si: n for n, si in enumerate(wstages)}
    si = 0
    while si < len(stages):
        st = stages[si]
        if len(st) > 3:
            grp = [si]
            while grp[-1] + 1 < len(stages) and len(stages[grp[-1] + 1]) > 3 and stages[grp[-1] + 1][3] == st[3]:
                grp.append(grp[-1] + 1)

            def mk(sj):
                def g():
                    plan_dma(widx[sj] + NSLOT - 2)
                    yield from stages[sj][1](slot_of[sj])
                return g()
            pipeline([mk(sj) for sj in grp], depth=2, offset=24)
            si = grp[-1] + 1
            continue
        if st[0] is not None:
            plan_dma(widx[si] + NSLOT - 2)
            st[1](slot_of[si])
        else:
            st[1](None)
        si += 1
    S.add("sp", lambda e: e.nop(), reads=[("outdram", m) for m in range(KC)])
    S.emit()
    return nc


_CACHE = {}


def _consts():
    c = np.zeros((128, 5, 128), np.float32)
    c[:, 0, :] = np.eye(128, dtype=np.float32)
    c[:, 1, :] = 1.0
    for blk in range(2):
        for i in range(32):
            c[blk * 64 + i + 32, 2, blk * 64 + i] = -1.0
            c[blk * 64 + i, 2, blk * 64 + i + 32] = 1.0
    s = np.arange(128)[:, None]
    t = np.arange(128)[None, :]
    c[:, 3, :] = ((s // 64 == t // 64) & (s <= t)).astype(np.float32)
    c[:, 4, :] = (s // 64 == t // 64).astype(np.float32)
    rmask = np.ones((128, 512), np.float32)
    rmask[:, ::64] = 0.0
    inv_freq = (np.float32(10000.0) ** (-np.arange(32, dtype=np.float32) * np.float32(2.0) / np.float32(64))).astype(np.float32)
    ropec = np.tile(inv_freq, 4).reshape(128, 1).astype(np.float32)
    j = np.arange(128)[:, None]
    i = np.arange(128)[None, :]
    prev = (j > i).astype(np.float32)
    cur = (j <= i).astype(np.float32)
    am = np.zeros((128, 2, 2, 4, 128), np.float32)
    am[:, 0, 0] = prev[:, None, :]
    am[:, 0, 1] = cur[:, None, :]
    am[:, 1, 0] = prev[:, None, :]
    am[:, 1, 1] = cur[:, None, :]
    return c, rmask, ropec, am.reshape(128, 2, 1024)


def kernel(x, positions, lb_table, ffn1_norm, ffn1_w_gu, ffn1_w_down, mix_norm, w_in, q_norm, k_norm,
           sinks, hg_out_norm, w_attn_branch, w_hg_branch, w_out, ffn2_norm, ffn2_w_gu, ffn2_w_down,
           _n_tiles=9, _cores=None):
    f = lambda a: np.ascontiguousarray(np.asarray(a), dtype=np.float32)
    x = f(x)
    positions = np.ascontiguousarray(np.asarray(positions), dtype=np.int32)
    key = _n_tiles
    if key not in _CACHE:
        _CACHE[key] = build(_n_tiles)
    nc = _CACHE[key]
    cm, rmask, ropec, am = _consts()
    gains = np.stack([f(ffn1_norm)[0].reshape(KC, 128).T, f(mix_norm)[0].reshape(KC, 128).T,
                      f(ffn2_norm)[0].reshape(KC, 128).T], axis=1)
    qkg = np.stack([np.tile(f(q_norm)[0], 2), np.tile(f(k_norm)[0], 2)], axis=1)
    hgg = f(hg_out_norm)[0].reshape(128, 1)
    lbt = f(lb_table).reshape(2, 8, 128).transpose(2, 0, 1)
    shared = {
        "wgu1": f(ffn1_w_gu)[0], "wgu2": f(ffn2_w_gu)[0], "wd1": f(ffn1_w_down)[0], "wd2": f(ffn2_w_down)[0],
        "win": f(w_in)[0], "wA": f(w_attn_branch)[0], "wR": f(w_hg_branch)[0], "wO": f(w_out)[0],
        "gains": np.ascontiguousarray(gains), "qkg": np.ascontiguousarray(qkg), "hgg": np.ascontiguousarray(hgg),
        "lbt": np.ascontiguousarray(lbt), "sinks": f(sinks).reshape(1, 16),
        "cmat": cm, "rmask": rmask, "ropec": ropec,
    }
    cores = list(range(8)) if _cores is None else _cores
    in_maps = []
    for cid in cores:
        b = cid // 4
        s0 = (cid % 4) * SEG
        xt = np.zeros((D, NT), np.float32)
        ps = np.zeros((1, NT), np.int32)
        if s0 > 0:
            xt[:, :] = x[b, s0 - HALO:s0 + SEG, :].T
            ps[0, :] = positions[b, s0 - HALO:s0 + SEG]
        else:
            xt[:, HALO:] = x[b, 0:SEG, :].T
            ps[0, HALO:] = positions[b, 0:SEG]
        am_c = am.copy()
        if s0 == 0:
            am_c[:, 1, 0:512] = 0.0
        m = dict(shared)
        m["xT"] = xt
        m["pos"] = ps
        m["amask"] = am_c
        in_maps.append(m)
    res = run_bass_kernel_spmd(nc, in_maps, core_ids=list(range(len(cores))))
    out = np.zeros((2, SEQ, D), np.float32)
    for k, cid in enumerate(cores):
        b = cid // 4
        s0 = (cid % 4) * SEG
        out[b, s0:s0 + SEG, :] = res.results[k]["outT"].T
    return out
```
